# Optimizing a Trainium2 kernel written in Bass

```python
import math
import jax, jax.numpy as jnp
from jax import lax
import numpy as np

D_MODEL = 1024
BATCH = 8
SEQ = 2048
DEPTH = 2

MOBA_HEADS = 8
HEAD_DIM = 64
MOBA_BLOCK = 256
MOBA_TOPK = 3
MOBA_Q_CHUNK = 16
MLA_HEADS = 8
MLA_Q_LORA = 384
MLA_KV_LORA = 256
MLA_NOPE = 64
MLA_ROPE = 32
MLA_V = 64
ROPE_THETA = 10000.0
DIFF_HEADS = 4
DIFF_QK = 64
DIFF_V = 2 * DIFF_QK
PLE_DIM = 256
Q_BLOCK = 128
NORM_EPS = 1e-5
NEG = -1e30

MOBA_W = MOBA_HEADS * HEAD_DIM
MLA_W = MLA_HEADS * MLA_V
DIFF_W = DIFF_HEADS * DIFF_V
DIFF_QK_W = DIFF_HEADS * 2 * DIFF_QK
IN_SIZES = (MOBA_W, MOBA_W, MOBA_W, MOBA_W,
            MLA_Q_LORA, MLA_KV_LORA, MLA_ROPE, MLA_W,
            DIFF_QK_W, DIFF_QK_W, DIFF_W, DIFF_W)
IN_WIDTH = sum(IN_SIZES)
N_BRANCH = 3
ALPHA = (2 * DEPTH) ** 0.25
BETA = (8 * DEPTH) ** -0.25

kernel_name = 'hybrid_moba_mla_diff_gated_deepnorm'


def _split_points():
    return [int(v) for v in np.cumsum(np.array(IN_SIZES))[:-1]]


def _alibi_slopes():
    n = MOBA_HEADS + DIFF_HEADS
    s = 2.0 ** (-8.0 * (np.arange(n) + 1) / n)
    diff_idx = np.arange(DIFF_HEADS) * (n // DIFF_HEADS)
    moba_idx = np.setdiff1d(np.arange(n), diff_idx)
    return (jnp.asarray(s[moba_idx], dtype=jnp.float32),
            jnp.asarray(s[diff_idx], dtype=jnp.float32))


def _rmsnorm(x, g, eps=1e-6):
    xf = x.astype(jnp.float32)
    y = xf * lax.rsqrt(jnp.mean(xf * xf, axis=-1, keepdims=True) + eps)
    return (y * g.astype(jnp.float32)).astype(x.dtype)


def _layernorm(x, g, b):
    xf = x.astype(jnp.float32)
    mu = jnp.mean(xf, axis=-1, keepdims=True)
    var = jnp.mean(jnp.square(xf - mu), axis=-1, keepdims=True)
    y = (xf - mu) * lax.rsqrt(var + NORM_EPS)
    return (y * g.astype(jnp.float32) + b.astype(jnp.float32)).astype(x.dtype)


def _rope(t, pos):
    d = t.shape[-1]
    freqs = ROPE_THETA ** (-jnp.arange(0, d, 2, dtype=jnp.float32) / d)
    ang = pos.astype(jnp.float32)[:, None] * freqs[None, :]
    cos, sin = jnp.cos(ang), jnp.sin(ang)
    tf = t.astype(jnp.float32)
    t1, t2 = tf[..., : d // 2], tf[..., d // 2:]
    return jnp.concatenate([t1 * cos - t2 * sin, t1 * sin + t2 * cos], axis=-1).astype(t.dtype)


def _split_heads(t, n):
    B, S, _ = t.shape
    return t.reshape(B, S, n, -1).transpose(0, 2, 1, 3)


def _merge_heads(t):
    B, H, S, d = t.shape
    return t.transpose(0, 2, 1, 3).reshape(B, S, H * d)


def moba_attention(q, k, v, slopes):
    B, H, S, dh = q.shape
    blk = MOBA_BLOCK
    nb = -(-S // blk)
    s_pad = nb * blk
    padw = ((0, 0), (0, 0), (0, s_pad - S), (0, 0))
    kp = jnp.pad(k, padw)
    vp = jnp.pad(v, padw)
    kb = kp.reshape(B, H, nb, blk, dh)
    vb = vp.reshape(B, H, nb, blk, dh)
    kmean = jnp.mean(kb.astype(jnp.float32), axis=3)
    pos = jnp.arange(S)
    gate = jnp.einsum('bhsd,bhnd->bhsn', q.astype(jnp.float32), kmean)
    past = jnp.arange(nb)[None, :] < (pos // blk)[:, None]
    gate = jnp.where(past, gate, -jnp.inf)
    kk = min(MOBA_TOPK, nb)
    _, sel = lax.top_k(gate, kk)

    C = MOBA_Q_CHUNK
    nc = S // C
    qc = q.reshape(B, H, nc, C, dh).transpose(2, 0, 1, 3, 4)
    selc = sel.reshape(B, H, nc, C, kk).transpose(2, 0, 1, 3, 4)
    bi = jnp.arange(B)[:, None, None, None]
    hi = jnp.arange(H)[None, :, None, None]
    scale = dh ** -0.5
    offs = jnp.arange(blk)

    def one(args):
        qi, si, c = args
        qpos = c * C + jnp.arange(C)
        ob = (c * C) // blk
        kg = kb[bi, hi, si]
        vg = vb[bi, hi, si]
        s_sel = jnp.einsum('bhqd,bhqjkd->bhqjk', qi, kg).astype(jnp.float32) * scale
        kpos_sel = si[..., None] * blk + offs
        dist_sel = (qpos[None, None, :, None, None] - kpos_sel).astype(jnp.float32)
        s_sel = s_sel - slopes[None, :, None, None, None] * dist_sel
        valid = jnp.arange(kk)[None, :] < (qpos // blk)[:, None]
        s_sel = jnp.where(valid[None, None, :, :, None], s_sel, NEG)
        ko = lax.dynamic_slice_in_dim(kp, ob * blk, blk, axis=2)
        vo = lax.dynamic_slice_in_dim(vp, ob * blk, blk, axis=2)
        dist_own = (qpos[:, None] - (ob * blk + offs)[None, :]).astype(jnp.float32)
        s_own = jnp.einsum('bhqd,bhkd->bhqk', qi, ko).astype(jnp.float32) * scale
        s_own = s_own - slopes[None, :, None, None] * dist_own
        s_own = jnp.where(dist_own >= 0, s_own, NEG)
        s_all = jnp.concatenate([s_sel.reshape(B, H, C, kk * blk), s_own], axis=-1)
        pr = jax.nn.softmax(s_all, axis=-1).astype(v.dtype)
        p_sel = pr[..., : kk * blk].reshape(B, H, C, kk, blk)
        p_own = pr[..., kk * blk:]
        return (jnp.einsum('bhqjk,bhqjkd->bhqd', p_sel, vg)
                + jnp.einsum('bhqk,bhkd->bhqd', p_own, vo))

    out = lax.map(one, (qc, selc, jnp.arange(nc)))
    return out.transpose(1, 2, 0, 3, 4).reshape(B, H, S, dh)


def mla_attention(q, k, v):
    B, H, S, dq = q.shape
    nq = S // Q_BLOCK
    scale = dq ** -0.5
    qb = q.reshape(B, H, nq, Q_BLOCK, dq).transpose(2, 0, 1, 3, 4)
    kpos = jnp.arange(S)

    def one(args):
        qi, i = args
        qpos = i * Q_BLOCK + jnp.arange(Q_BLOCK)
        s = jnp.einsum('bhqd,bhkd->bhqk', qi, k).astype(jnp.float32) * scale
        s = jnp.where(kpos[None, :] <= qpos[:, None], s, NEG)
        pr = jax.nn.softmax(s, axis=-1).astype(v.dtype)
        return jnp.einsum('bhqk,bhkd->bhqd', pr, v)

    out = lax.map(one, (qb, jnp.arange(nq)))
    return out.transpose(1, 2, 0, 3, 4).reshape(B, H, S, -1)


def diff_attention(q, k, v, slopes, lam):
    B, H, S = q.shape[:3]
    nq = S // Q_BLOCK
    scale = DIFF_QK ** -0.5
    qb = q.reshape(B, H, nq, Q_BLOCK, 2, DIFF_QK).transpose(2, 0, 1, 3, 4, 5)
    kpos = jnp.arange(S)

    def one(args):
        qi, i = args
        qpos = i * Q_BLOCK + jnp.arange(Q_BLOCK)
        dist = (qpos[:, None] - kpos[None, :]).astype(jnp.float32)
        s = jnp.einsum('bhqcd,bhkcd->bhcqk', qi, k).astype(jnp.float32) * scale
        s = s - slopes[None, :, None, None, None] * dist
        s = jnp.where(dist >= 0, s, NEG)
        pr = jax.nn.softmax(s, axis=-1)
        a = (pr[:, :, 0] - lam * pr[:, :, 1]).astype(v.dtype)
        return jnp.einsum('bhqk,bhkd->bhqd', a, v)

    out = lax.map(one, (qb, jnp.arange(nq)))
    return out.transpose(1, 2, 0, 3, 4).reshape(B, H, S, -1)


def _layer(x, p_i, li, w_in, gq, gkv, w_uq, w_ukv, lam_p, subln_g, w_a, w_b, w_c,
           w_m, b_m, w_o, ln_g, ln_b, w_pg, w_p, slopes_a, slopes_c):
    B, S, D = x.shape
    pos = jnp.arange(S)
    h = x @ w_in
    (a_q, a_k, a_v, a_z, b_cq, b_ckv, b_kr, b_z,
     c_q, c_k, c_v, c_z) = jnp.split(h, _split_points(), axis=-1)

    ya = moba_attention(_split_heads(a_q, MOBA_HEADS), _split_heads(a_k, MOBA_HEADS),
                        _split_heads(a_v, MOBA_HEADS), slopes_a)
    ya = _merge_heads(ya) * jax.nn.silu(a_z)

    cq = _rmsnorm(b_cq, gq)
    ckv = _rmsnorm(b_ckv, gkv)
    qh = _split_heads(cq @ w_uq, MLA_HEADS)
    q_b = jnp.concatenate([qh[..., :MLA_NOPE], _rope(qh[..., MLA_NOPE:], pos)], axis=-1)
    kvh = _split_heads(ckv @ w_ukv, MLA_HEADS)
    k_rope = _rope(b_kr[:, None], pos)
    k_b = jnp.concatenate([kvh[..., :MLA_NOPE],
                           jnp.broadcast_to(k_rope, (B, MLA_HEADS, S, MLA_ROPE))], axis=-1)
    yb = mla_attention(q_b, k_b, kvh[..., MLA_NOPE:])
    yb = _merge_heads(yb) * jax.nn.silu(b_z)

    lam_init = 0.8 - 0.6 * math.exp(-0.3 * li)
    lf = lam_p.astype(jnp.float32)
    lam = (jnp.exp(jnp.sum(lf[0] * lf[1])) - jnp.exp(jnp.sum(lf[2] * lf[3])) + lam_init)
    qc = c_q.reshape(B, S, DIFF_HEADS, 2, DIFF_QK).transpose(0, 2, 1, 3, 4)
    kc = c_k.reshape(B, S, DIFF_HEADS, 2, DIFF_QK).transpose(0, 2, 1, 3, 4)
    yc = diff_attention(qc, kc, _split_heads(c_v, DIFF_HEADS), slopes_c, lam)
    yc = _rmsnorm(yc, subln_g, eps=1e-5) * (1.0 - lam_init)
    yc = _merge_heads(yc) * jax.nn.silu(c_z)

    g = jax.nn.sigmoid(x @ w_m + b_m)
    ga, gb, gc = jnp.split(g, N_BRANCH, axis=-1)
    merged = ga * (ya @ w_a) + gb * (yb @ w_b) + gc * (yc @ w_c)
    out = merged @ w_o

    r = ALPHA * x + out
    r = r + jax.nn.sigmoid(r @ w_pg) * (p_i @ w_p)
    return _layernorm(r, ln_g, ln_b)


def setup_inputs(seed: int = 0) -> dict:
    key = jax.random.key(seed)
    ks = jax.random.split(key, 20)
    f32 = jnp.float32

    def nrm(k, shape, scale):
        return jax.random.normal(k, shape, f32) * scale

    L, D = DEPTH, D_MODEL
    return {
        'x': nrm(ks[0], (BATCH, SEQ, D), 1.0),
        'p': nrm(ks[1], (L, BATCH, SEQ, PLE_DIM), 1.0),
        'w_in': nrm(ks[2], (L, D, IN_WIDTH), D ** -0.5),
        'mla_q_norm_g': 1.0 + nrm(ks[3], (L, MLA_Q_LORA), 0.01),
        'mla_kv_norm_g': 1.0 + nrm(ks[4], (L, MLA_KV_LORA), 0.01),
        'mla_w_uq': nrm(ks[5], (L, MLA_Q_LORA, MLA_HEADS * (MLA_NOPE + MLA_ROPE)), MLA_Q_LORA ** -0.5),
        'mla_w_ukv': nrm(ks[6], (L, MLA_KV_LORA, MLA_HEADS * (MLA_NOPE + MLA_V)), MLA_KV_LORA ** -0.5),
        'diff_lambda': nrm(ks[7], (L, 4, DIFF_QK), 0.1),
        'diff_subln_g': 1.0 + nrm(ks[8], (L, DIFF_V), 0.01),
        'w_branch_a': nrm(ks[9], (L, MOBA_W, D), BETA * MOBA_W ** -0.5),
        'w_branch_b': nrm(ks[10], (L, MLA_W, D), BETA * MLA_W ** -0.5),
        'w_branch_c': nrm(ks[11], (L, DIFF_W, D), BETA * DIFF_W ** -0.5),
        'w_merge': nrm(ks[12], (L, D, N_BRANCH * D), D ** -0.5),
        'b_merge': nrm(ks[13], (L, N_BRANCH * D), 0.01),
        'w_out': nrm(ks[14], (L, D, D), BETA * D ** -0.5),
        'ln_g': 1.0 + nrm(ks[15], (L, D), 0.01),
        'ln_b': nrm(ks[16], (L, D), 0.01),
        'w_ple_gate': nrm(ks[17], (L, D, D), D ** -0.5),
        'w_ple': nrm(ks[18], (L, PLE_DIM, D), BETA * PLE_DIM ** -0.5),
    }


def reference(x, p, w_in, mla_q_norm_g, mla_kv_norm_g, mla_w_uq, mla_w_ukv, diff_lambda,
              diff_subln_g, w_branch_a, w_branch_b, w_branch_c, w_merge, b_merge, w_out,
              ln_g, ln_b, w_ple_gate, w_ple):
    slopes_a, slopes_c = _alibi_slopes()
    h = x
    for i in range(DEPTH):
        h = _layer(h, p[i], i, w_in[i], mla_q_norm_g[i], mla_kv_norm_g[i], mla_w_uq[i],
                   mla_w_ukv[i], diff_lambda[i], diff_subln_g[i], w_branch_a[i], w_branch_b[i],
                   w_branch_c[i], w_merge[i], b_merge[i], w_out[i], ln_g[i], ln_b[i],
                   w_ple_gate[i], w_ple[i], slopes_a, slopes_c)
    return h
```

```python
import collections
import contextlib
import math
import numpy as np
import ml_dtypes
import concourse.bass as bass
import concourse.mybir as mybir
from concourse.bass_utils import run_bass_kernel_spmd

F32 = mybir.dt.float32
BF16 = mybir.dt.bfloat16
U8 = mybir.dt.uint8
AF = mybir.ActivationFunctionType
ALU = mybir.AluOpType
AX = mybir.AxisListType

S_LEN = 2048
D = 1024
DEPTH = 2
NCORES = 8
IN_W = 5280
KB = 1024
ALPHA = (2 * DEPTH) ** 0.25
NEGBIG = -30000.0
UPTO = 5

O_AQ, O_AK, O_AV, O_AZ = 0, 512, 1024, 1536
O_BCQ, O_BCKV, O_BKR, O_BZ = 2048, 2432, 2688, 2720
O_CQ, O_CK, O_CV, O_CZ = 3232, 3744, 4256, 4768

ENGS = ("pe", "act", "dve", "pool", "sp")
N_DMA_SEMS = 24


class _Rec:
    def __init__(self):
        self.calls = []

    def __getattr__(self, name):
        def f(*a, **k):
            self.calls.append((name, a, k))
            return self
        return f


def _freeze(fn):
    rec = _Rec()
    fn(rec)
    calls = rec.calls

    def replay(engine):
        ins = None
        for (name, a, k) in calls:
            ins = getattr(engine, name)(*a, **k)
        return ins
    return replay


class Sched:
    def __init__(self):
        self.q = {e: [] for e in ENGS}
        self.cnt = {e: 0 for e in ENGS}
        self.seen = {e: {} for e in ENGS}
        self.last_w = {}
        self.readers = {}
        self.dma_cnt = [0] * N_DMA_SEMS
        self.dma_rr = 0
        self.pending = {e: [] for e in ENGS}

    @staticmethod
    def _split(reads, writes):
        excl = [b for b in reads if b.startswith("P:")]
        reads = [b for b in reads if not b.startswith("P:")]
        return reads, list(writes) + excl

    def _deps(self, reads, writes):
        reads, writes = self._split(reads, writes)
        deps = []
        for b in reads:
            w = self.last_w.get(b)
            if w is not None:
                deps.append(w)
        for b in writes:
            w = self.last_w.get(b)
            if w is not None:
                deps.append(w)
            deps.extend(self.readers.get(b, ()))
        return deps

    def _waits(self, eng, deps):
        need = {}
        for (k, v) in deps:
            if v > need.get(k, 0):
                need[k] = v
        out = []
        for k, v in need.items():
            if self.seen[eng].get(k, 0) < v:
                self.seen[eng][k] = v
                out.append((k, v))
        return out

    def _commit(self, reads, writes, tok):
        reads, writes = self._split(reads, writes)
        for b in reads:
            self.readers.setdefault(b, []).append(tok)
        for b in writes:
            self.last_w[b] = tok
            self.readers[b] = []

    def barrier(self):
        toks = [(e, self.cnt[e]) for e in ENGS if self.cnt[e] > 0]
        toks += [(("dma", i), 16 * self.dma_cnt[i]) for i in range(N_DMA_SEMS) if self.dma_cnt[i] > 0]
        for e in ENGS:
            self.pending[e] = list(toks)

    def op(self, eng, fn, reads=(), writes=()):
        deps = self._deps(reads, writes) + self.pending[eng]
        self.pending[eng] = []
        if eng == "pe":
            deps = [d for d in deps if d[0] != "pe"]
        waits = self._waits(eng, deps)
        self.cnt[eng] += 1
        tok = (eng, self.cnt[eng])
        self.q[eng].append((waits, _freeze(fn), (eng, 1)))
        self._commit(reads, writes, tok)
        return tok

    def dma(self, eng, fn, reads=(), writes=()):
        s = self.dma_rr
        self.dma_rr = (self.dma_rr + 1) % N_DMA_SEMS
        deps = self._deps(reads, writes) + self.pending[eng]
        self.pending[eng] = []
        key = ("dma", s)
        if self.dma_cnt[s] > 0:
            deps.append((key, 16 * self.dma_cnt[s]))
        waits = self._waits(eng, deps)
        self.dma_cnt[s] += 1
        tok = (key, 16 * self.dma_cnt[s])
        self.q[eng].append((waits, _freeze(fn), (key, 16)))
        self._commit(reads, writes, tok)
        return tok

    def emit(self, nc, final_eng="sp"):
        with contextlib.ExitStack() as st:
            sems = {}
            for e in ENGS:
                sems[e] = st.enter_context(nc.semaphore("s_" + e))
            for i in range(N_DMA_SEMS):
                sems[("dma", i)] = st.enter_context(nc.semaphore("s_dma%d" % i))
            fin = [(e, self.cnt[e]) for e in ENGS if self.cnt[e] > 0]
            fin += [(("dma", i), 16 * self.dma_cnt[i]) for i in range(N_DMA_SEMS) if self.dma_cnt[i] > 0]
            block = st.enter_context(nc.Block())
            handles = {"pe": block.tensor, "act": block.scalar, "dve": block.vector,
                       "pool": block.gpsimd, "sp": block.sync}
            for e in ENGS:
                ops = self.q[e]
                is_final = (e == final_eng)
                if not ops and not is_final:
                    continue

                def body(engine, ops=ops, is_final=is_final):
                    for (waits, fn, inc) in ops:
                        for (k, v) in waits:
                            engine.wait_ge(sems[k], v)
                        ins = fn(engine)
                        ins.then_inc(sems[inc[0]], inc[1])
                    if is_final:
                        for (k, v) in fin:
                            engine.wait_ge(sems[k], v)
                handles[e](body)


def _bf16_split3(v):
    v = v.astype(np.float32)
    hi = v.astype(ml_dtypes.bfloat16).astype(np.float32)
    r = v - hi
    mid = r.astype(ml_dtypes.bfloat16).astype(np.float32)
    lo = (r - mid).astype(ml_dtypes.bfloat16).astype(np.float32)
    return hi, mid, lo


def make_consts():
    n = 12
    s = 2.0 ** (-8.0 * (np.arange(n) + 1) / n)
    diff_idx = np.arange(4) * 3
    moba_idx = np.setdiff1d(np.arange(n), diff_idx)
    slopes = np.concatenate([s[moba_idx], s[diff_idx]]).astype(np.float32)
    pos = np.arange(S_LEN, dtype=np.float32)
    scale = 0.125
    aq = np.zeros((12, 8, S_LEN), np.float32)
    ak = np.zeros((12, 8, S_LEN), np.float32)
    for h in range(12):
        c = np.float32(slopes[h]) / np.float32(scale)
        v = (c * pos).astype(np.float32)
        hi, mid, lo = _bf16_split3(-v)
        aq[h, 0], aq[h, 1], aq[h, 2] = hi, mid, lo
        aq[h, 3:6] = 1.0
        hi, mid, lo = _bf16_split3(v)
        ak[h, 0:3] = 1.0
        ak[h, 3], ak[h, 4], ak[h, 5] = hi, mid, lo
    ind = np.zeros((8, S_LEN), np.float32)
    for b in range(8):
        ind[b, b * 256:(b + 1) * 256] = 1.0
    ident = np.eye(128, dtype=np.float32)
    kk = np.arange(128)[:, None]
    qq = np.arange(128)[None, :]
    tri = np.where(kk > qq, NEGBIG, 0.0).astype(np.float32)
    freqs = (10000.0 ** (-np.arange(0, 32, 2, dtype=np.float32) / 32)).astype(np.float32)
    ang = pos[None, :] * freqs[:, None]
    cs = np.zeros((2, 32, S_LEN), np.float32)
    cs[0] = np.concatenate([np.cos(ang), np.cos(ang)], axis=0)
    cs[1] = np.concatenate([np.sin(ang), np.sin(ang)], axis=0)
    pen = np.zeros((128, 8, 8), np.float32)
    own = np.zeros((128, 8, 8), np.float32)
    for qi in range(8):
        qb = (8 + qi) // 2
        for b in range(8):
            pen[:, qi, b] = 0.0 if b < qb else -1e30
            own[:, qi, b] = 1.0 if b >= qb else 0.0
    pen2 = np.concatenate([pen.reshape(128, 64)] * 2, axis=1)
    own2 = np.concatenate([own.reshape(128, 64)] * 2, axis=1)
    return dict(c_aq=aq, c_ak=ak, c_ind=ind, c_ident=ident, c_tri=tri, c_cs=cs.astype(np.float32),
                c_pen=np.ascontiguousarray(pen2), c_own=np.ascontiguousarray(own2))


WNAMES = ["w_in", "mla_q_norm_g", "mla_kv_norm_g", "mla_w_uq", "mla_w_ukv", "diff_lambda", "diff_subln_g",
          "w_branch_a", "w_branch_b", "w_branch_c", "w_merge", "b_merge", "w_out", "ln_g", "ln_b",
          "w_ple_gate", "w_ple"]
WSHAPES = {"w_in": [DEPTH, D, IN_W], "mla_q_norm_g": [DEPTH, 384], "mla_kv_norm_g": [DEPTH, 256],
           "mla_w_uq": [DEPTH, 384, 768], "mla_w_ukv": [DEPTH, 256, 1024], "diff_lambda": [DEPTH, 4, 64],
           "diff_subln_g": [DEPTH, 128], "w_branch_a": [DEPTH, 512, D], "w_branch_b": [DEPTH, 512, D],
           "w_branch_c": [DEPTH, 512, D], "w_merge": [DEPTH, D, 3 * D], "b_merge": [DEPTH, 3 * D],
           "w_out": [DEPTH, D, D], "ln_g": [DEPTH, D], "ln_b": [DEPTH, D], "w_ple_gate": [DEPTH, D, D],
           "w_ple": [DEPTH, 256, D]}
CSHAPES = {"c_aq": [12, 8, S_LEN], "c_ak": [12, 8, S_LEN], "c_ind": [8, S_LEN], "c_ident": [128, 128],
           "c_tri": [128, 128], "c_cs": [2, 32, S_LEN], "c_pen": [128, 128], "c_own": [128, 128]}


def build_program(layers, debug=()):
    nc = bass.Bass("TRN2", target_bir_lowering=False)
    x_in = nc.dram_tensor("x", [S_LEN, D], F32, kind="ExternalInput").ap()
    p_in = nc.dram_tensor("p", [DEPTH, S_LEN, 256], F32, kind="ExternalInput").ap()
    W = {n: nc.dram_tensor(n, WSHAPES[n], F32, kind="ExternalInput").ap() for n in WNAMES}
    C = {n: nc.dram_tensor(n, CSHAPES[n], F32, kind="ExternalInput").ap() for n in CSHAPES}
    out = nc.dram_tensor("out", [S_LEN, D], F32, kind="ExternalOutput").ap()
    scr = None
    if len(layers) > 1:
        scr = nc.dram_tensor("scr", [S_LEN, D], F32, kind="Internal").ap()
    dbg = {}
    for (name, shape) in debug:
        dbg[name] = nc.dram_tensor(name, shape, F32, kind="ExternalOutput").ap()

    S = Sched()
    with contextlib.ExitStack() as st:
        arena = st.enter_context(nc.sbuf_tensor("arena", [128, 206 * KB], U8))
        psall = st.enter_context(nc.psum_tensor("psall", [128, 4096], F32))

        def V(off, dtype, shape):
            es = 2 if dtype == BF16 else 4
            n = int(np.prod(shape))
            assert off + n * es <= 206 * KB, (off, n * es)
            ap = arena[:, off:off + n * es].bitcast(dtype)
            if len(shape) == 2:
                ap = ap.rearrange("p (a b) -> p a b", a=shape[0], b=shape[1])
            elif len(shape) == 3:
                ap = ap.rearrange("p (a b c) -> p a b c", a=shape[0], b=shape[1], c=shape[2])
            return ap

        def PB(i):
            return psall[:, i * 512:(i + 1) * 512]

        def PBb(i):
            return psall[:, i * 512:(i + 1) * 512].bitcast(BF16)

        xT = V(0, BF16, [8, S_LEN])
        pT = V(32 * KB, BF16, [2, S_LEN])
        ybT = V(40 * KB, BF16, [4, S_LEN])
        yaT = V(56 * KB, BF16, [4, S_LEN])
        ycT = V(72 * KB, BF16, [4, S_LEN])
        CO = 88 * KB
        ident = V(CO, BF16, [128])
        tri = V(CO + 256, BF16, [128])
        ones = V(CO + 512, BF16, [128])
        pen = V(CO + 768, F32, [128])
        own = V(CO + 1280, F32, [128])
        epsb = V(CO + 1792, F32, [8])
        hb = V(CO + 1824, F32, [24])
        PH = 90 * KB

        cp_rr = [0]

        def evac(out_ap, in_ap, reads, writes, eng=None):
            if eng is None:
                eng = ("act", "dve")[cp_rr[0] % 2]
                cp_rr[0] += 1
            if eng == "act":
                S.op("act", lambda e: e.activation(out=out_ap, in_=in_ap, func=AF.Copy), reads=reads, writes=writes)
            else:
                S.op("dve", lambda e: e.tensor_copy(out=out_ap, in_=in_ap), reads=reads, writes=writes)

        def mm_group(mms, reads, writes):
            def fn(e, mms=mms):
                ins = None
                n = len(mms)
                for i, (o, l, r) in enumerate(mms):
                    ins = e.matmul(o, l, r, start=(i == 0), stop=(i == n - 1))
                return ins
            S.op("pe", fn, reads=reads, writes=writes)

        def load_consts():
            S.dma("pool", lambda e: e.dma_start(out=ident, in_=C["c_ident"]), writes=["ident"])
            S.dma("pool", lambda e: e.dma_start(out=tri, in_=C["c_tri"]), writes=["tri"])
            S.op("pool", lambda e: e.memset(ones, 1.0), writes=["ones"])
            S.dma("sp", lambda e: e.dma_start(out=pen, in_=C["c_pen"]), writes=["pen"])
            S.dma("sp", lambda e: e.dma_start(out=own, in_=C["c_own"]), writes=["own"])
            for i, v in enumerate([384.0 * 1e-6, 256.0 * 1e-6, 128.0 * 1e-5, 1e-5 / (ALPHA * ALPHA)]):
                S.op("pool", lambda e, i=i, v=v: e.memset(epsb[:, i:i + 1], float(v)), writes=["epsb"])

        def load_piece(buf, src_ap, kc, c0, n, name):
            S.dma("pool", lambda e: e.dma_start(out=buf[:, 0:kc, c0:c0 + n], in_=src_ap.rearrange("(c p) n -> p c n", p=128)),
                  writes=[name])

        wc_rr = [0]

        def load_wchunk(wbufs, src_ap, kc, ncols, tagbase):
            i = wc_rr[0] % len(wbufs)
            wc_rr[0] += 1
            buf = wbufs[i]
            name = "%s%d" % (tagbase, i)
            dst = buf[:, 0:kc, 0:ncols]
            S.dma("pool", lambda e: e.dma_start(out=dst, in_=src_ap.rearrange("(c p) n -> p c n", p=128)),
                  writes=[name])
            return buf, name

        def layer(li, xsrc, ydst):
            lam_init = 0.8 - 0.6 * math.exp(-0.3 * li)
            w_in = W["w_in"][li]

            S.barrier()
            def phase0():
                NXB = 3
                xb = [V(PH + 108 * KB + i * 2 * KB, BF16, [D]) for i in range(NXB)]
                pb = [V(PH + 114 * KB + i * 512, BF16, [256]) for i in range(NXB)]
                for tt in range(16):
                    b = xb[tt % NXB]
                    bn = "xb%d" % (tt % NXB)
                    S.dma("pool", lambda e, b=b, tt=tt: e.dma_start(out=b, in_=xsrc[tt * 128:(tt + 1) * 128, :]),
                          reads=(["Y%d" % tt] if xsrc is not x_in else []), writes=[bn])
                    bk = tt % 2

                    def tr(e, b=b, bk=bk):
                        ins = None
                        for c in range(8):
                            ins = e.transpose(out=PBb(bk)[:, c * 128:(c + 1) * 128], in_=b[:, c * 128:(c + 1) * 128],
                                              identity=ident)
                        return ins
                    S.op("pe", tr, reads=[bn, "ident"], writes=["P:%d" % bk])
                    evac(xT[:, :, tt * 128:(tt + 1) * 128],
                         PBb(bk).rearrange("p (c t) -> p c t", c=8), ["P:%d" % bk], ["xT"])
                    b2 = pb[tt % NXB]
                    b2n = "pb%d" % (tt % NXB)
                    S.dma("pool", lambda e, b2=b2, tt=tt: e.dma_start(out=b2, in_=p_in[li, tt * 128:(tt + 1) * 128, :]),
                          writes=[b2n])
                    bk2 = 2 + tt % 2

                    def tr2(e, b2=b2, bk2=bk2):
                        ins = None
                        for c in range(2):
                            ins = e.transpose(out=PBb(bk2)[:, c * 128:(c + 1) * 128], in_=b2[:, c * 128:(c + 1) * 128],
                                              identity=ident)
                        return ins
                    S.op("pe", tr2, reads=[b2n, "ident"], writes=["P:%d" % bk2])
                    evac(pT[:, :, tt * 128:(tt + 1) * 128],
                         PBb(bk2)[:, 0:256].rearrange("p (c t) -> p c t", c=2), ["P:%d" % bk2], ["pT"])


            def attention(nm, QT, KT, kparts, Vfn, scale, Qs, accs, outfn, pts, sb=((0, 1), (2, 3)), fill=None, LA=1, grp=2, fill_every=1):
                jobs = []
                for ji, Q in enumerate(Qs):
                    if isinstance(nm, list):
                        tag, QTj, KTj, qnj, knj, kpj = nm[ji]
                    else:
                        tag, qnj, knj = nm
                        QTj, KTj = QT, KT
                        kpj = kparts
                    nkt = 4 * Q + 4
                    abk = accs(ji if isinstance(nm, list) else Q)
                    for kt in range(nkt):
                        jobs.append((ji, Q, kt, nkt, abk, kpj, QTj, KTj, qnj, knj))
                n = len(jobs)
                units = []
                i = 0
                while i < n:
                    (ji, Q, kt, nkt) = jobs[i][0:4]
                    if grp == 2 and kt - 4 * Q < 0 and i + 1 < n and jobs[i + 1][0] == ji and jobs[i + 1][2] - 4 * Q < 0:
                        units.append([i, i + 1]); i += 2
                    else:
                        units.append([i]); i += 1
                nu = len(units)

                def score(u):
                    slot = sb[u % len(sb)]
                    pt = pts[u % len(pts)]
                    ptn = "pt%d" % (u % len(pts))
                    for idx, ti in enumerate(units[u]):
                        (ji, Q, kt, nkt, abk, (lo, hi), QTj, KTj, qnj, knj) = jobs[ti]
                        j = kt - 4 * Q
                        c0 = max(j, 0) * 128
                        sbk = slot[idx]
                        mms = [(PB(sbk)[:, c0:512], KTj[lo:hi, kt * 128:(kt + 1) * 128],
                                QTj[lo:hi, Q * 512 + c0:(Q + 1) * 512])]
                        rd = [qnj, knj]
                        if j >= 0:
                            mms.append((PB(sbk)[:, c0:c0 + 128], ident, tri))
                            rd += ["ident", "tri"]
                        mm_group(mms, rd, ["P:%d" % sbk])
                    if len(units[u]) == 2:
                        b0 = slot[0]
                        S.op("act", lambda e: e.activation(out=pt[:, 0:1024], in_=psall[:, b0 * 512:b0 * 512 + 1024],
                                                           func=AF.Exp, scale=scale),
                             reads=["P:%d" % slot[0], "P:%d" % slot[1]], writes=[ptn])
                    else:
                        S.op("act", lambda e: e.activation(out=pt[:, c0:512], in_=PB(sbk)[:, c0:512], func=AF.Exp, scale=scale),
                             reads=["P:%d" % sbk], writes=[ptn])

                def pvs(u):
                    pt = pts[u % len(pts)]
                    ptn = "pt%d" % (u % len(pts))
                    for idx, ti in enumerate(units[u]):
                        (ji, Q, kt, nkt, abk, kpj, QTj, KTj, qnj, knj) = jobs[ti]
                        j = kt - 4 * Q
                        c0 = max(j, 0) * 128
                        for (lhsT, lname), obk in zip(Vfn(ji if isinstance(nm, list) else Q, kt), abk):
                            S.op("pe", lambda e: e.matmul(PB(obk)[:, c0:512], lhsT, pt[:, idx * 512 + c0:(idx + 1) * 512],
                                                          start=(kt == 0), stop=(kt == nkt - 1)),
                                 reads=[ptn, lname], writes=["P:%d" % obk])
                        if kt == nkt - 1:
                            outfn(ji if isinstance(nm, list) else Q, abk)

                fcnt = [0]
                for u in range(nu + LA):
                    if u < nu:
                        score(u)
                    if fill is not None:
                        for _ in range(len(units[min(u, nu - 1)])):
                            fcnt[0] += 1
                            if fcnt[0] % fill_every == 0:
                                next(fill, None)
                    if u - LA >= 0:
                        pvs(u - LA)

            M0 = 56 * KB
            cqn = V(M0, BF16, [3, S_LEN])
            ckvn = V(M0 + 12 * KB, BF16, [2, S_LEN])
            krope = V(M0 + 20 * KB, BF16, [S_LEN])
            wukv = V(M0 + 24 * KB, BF16, [2, 1024])
            wv = V(M0 + 28 * KB, BF16, [2, 512])
            o = PH
            cs = V(o, F32, [2, S_LEN]); o += 16 * KB
            wuq = V(o, BF16, [3, 768]); o += 4608
            wuqr = V(o, BF16, [3, 768]); o += 4608
            vaug = V(o, BF16, [16, 4, 128]); o += 16 * KB
            QTs = [V(o + i * 4 * KB, BF16, [S_LEN]) for i in range(2)]; o += 8 * KB
            KTs = [V(o + i * 4 * KB, BF16, [S_LEN]) for i in range(2)]; o += 8 * KB
            pts = [V(o + i * KB, BF16, [512]) for i in range(4)]; o += 4 * KB
            wcs = [V(o + i * 8 * KB, BF16, [8, 512]) for i in range(2)]; o += 16 * KB
            sz = V(o, BF16, [S_LEN]); o += 4 * KB
            szs_b = [sz, V(o, BF16, [S_LEN])]; o += 4 * KB
            rs = [V(o + i * 2 * KB, F32, [512]) for i in range(2)]; o += 4 * KB
            rstd = V(o, F32, [512]); o += 2 * KB
            sq = V(o, BF16, [3, 512]); o += 3 * KB
            tmpa = V(o, F32, [512]); o += 2 * KB
            tmpb = V(o, F32, [512]); o += 2 * KB
            tmpc = V(o, F32, [512]); o += 2 * KB
            wkr = V(o, BF16, [8, 96]); o += 1536
            wkrr = V(o, BF16, [8, 96]); o += 1536
            gq = V(o, F32, [3]); o += 32
            gkv = V(o, F32, [2]); o += 32
            stg = V(o, F32, [1024]); o += 4 * KB
            assert o <= PH + 108 * KB, o

            wA = wcs[0]; wB = wcs[1]; wBn = "wB"
            for c in range(4):
                load_piece(wA, w_in[:, O_BCQ + c * 128:O_BCQ + (c + 1) * 128], 8, c * 128, 128, "wA%d" % c)
            load_piece(wB, w_in[:, O_BCQ + 512:O_BCQ + 672], 8, 0, 160, "wB")
            S.dma("sp", lambda e: e.dma_start(out=hb, in_=W["b_merge"][li].rearrange("(c p) -> p c", p=128),
                                              allow_slow_non_contiguous=True), writes=["hb"])
            S.op("dve", lambda e: e.tensor_scalar(out=hb, in0=hb, scalar1=0.5, scalar2=None, op0=ALU.mult), reads=["hb"], writes=["hb"])
            phase0()
            S.dma("sp", lambda e: e.dma_start(out=cs[64:96, :, :], in_=C["c_cs"].rearrange("a f s -> f a s")),
                  writes=["cs"])
            S.dma("sp", lambda e: e.dma_start(out=gq, in_=W["mla_q_norm_g"][li].rearrange("(c p) -> p c", p=128),
                                              allow_slow_non_contiguous=True), writes=["gq"])
            S.dma("sp", lambda e: e.dma_start(out=gkv, in_=W["mla_kv_norm_g"][li].rearrange("(c p) -> p c", p=128),
                                              allow_slow_non_contiguous=True), writes=["gkv"])
            S.op("dve", lambda e: e.tensor_scalar(out=gq, in0=gq, scalar1=float(math.sqrt(384.0)), scalar2=None, op0=ALU.mult),
                 reads=["gq"], writes=["gq"])
            S.op("dve", lambda e: e.tensor_scalar(out=gkv, in0=gkv, scalar1=float(math.sqrt(256.0)), scalar2=None, op0=ALU.mult),
                 reads=["gkv"], writes=["gkv"])
            for c in range(3):
                S.dma("sp", lambda e, c=c: e.dma_start(out=stg[:, 0:768], in_=W["mla_w_uq"][li, c * 128:(c + 1) * 128, :]),
                      writes=["stg"])
                S.op("dve", lambda e, c=c: e.tensor_scalar(out=wuq[:, c, :], in0=stg[:, 0:768], scalar1=gq[:, c:c + 1],
                                                         scalar2=None, op0=ALU.mult),
                     reads=["stg", "gq"], writes=["wuq"])
            for c in range(2):
                S.dma("sp", lambda e, c=c: e.dma_start(out=stg[:, 0:1024], in_=W["mla_w_ukv"][li, c * 128:(c + 1) * 128, :]),
                      writes=["stg"])
                S.op("dve", lambda e, c=c: e.tensor_scalar(out=wukv[:, c, :], in0=stg[:, 0:1024], scalar1=gkv[:, c:c + 1],
                                                         scalar2=None, op0=ALU.mult),
                     reads=["stg", "gkv"], writes=["wukv"])
            S.op("pool", lambda e: e.memset(wuqr, 0.0), writes=["wuqr"])
            for c in range(3):
                src = wuq[:, c, :].rearrange("p (h f) -> p h f", h=8)
                dst = wuqr[:, c, :].rearrange("p (h f) -> p h f", h=8)
                S.op("pool", lambda e, src=src, dst=dst: e.tensor_scalar(out=dst[:, :, 64:80], in0=src[:, :, 80:96],
                                                                        scalar1=-1.0, scalar2=None, op0=ALU.mult),
                     reads=["wuq"], writes=["wuqr"])
                S.op("pool", lambda e, src=src, dst=dst: e.tensor_copy(out=dst[:, :, 80:96], in_=src[:, :, 64:80]),
                     reads=["wuq"], writes=["wuqr"])
            for c in range(2):
                src = wukv[:, c, :].rearrange("p (h f) -> p h f", h=8)
                dst = wv[:, c, :].rearrange("p (h f) -> p h f", h=8)
                S.op("pool", lambda e, src=src, dst=dst: e.tensor_copy(out=dst, in_=src[:, :, 64:128]),
                     reads=["wukv"], writes=["wv"])

            S.op("pool", lambda e: e.memset(wkr, 0.0), writes=["wkr"])
            S.op("pool", lambda e: e.memset(wkrr, 0.0), writes=["wkrr"])
            S.op("pool", lambda e: e.tensor_copy(out=wkr[:, :, 64:96], in_=wB[:, :, 128:160]), reads=[wBn], writes=["wkr"])
            S.op("pool", lambda e: e.tensor_scalar(out=wkrr[:, :, 64:80], in0=wB[:, :, 144:160], scalar1=-1.0,
                                                  scalar2=None, op0=ALU.mult), reads=[wBn], writes=["wkrr"])
            S.op("pool", lambda e: e.tensor_copy(out=wkrr[:, :, 80:96], in_=wB[:, :, 128:144]), reads=[wBn], writes=["wkrr"])

            def rms_group(T, nch, wsel, dst, eps_n):
                for c in range(nch):
                    wbuf, wn, col = wsel(c)
                    mm_group([(PB(c), wbuf[:, k, col:col + 128], xT[:, k, T * 512:(T + 1) * 512]) for k in range(8)],
                             [wn, "xT"], ["P:%d" % c])
                    S.op("act", lambda e, c=c: e.activation(out=sq[:, c, :], in_=PB(c), func=AF.Square),
                         reads=["P:%d" % c], writes=["sq%d" % c])
                mm_group([(PB(3), ones, sq[:, c, :]) for c in range(nch)], ["ones"] + ["sq%d" % c for c in range(nch)], ["P:3"])
                S.op("act", lambda e: e.activation(out=rstd, in_=PB(3), func=AF.Sqrt, bias=epsb[:, 0:1] if eps_n > 3e-4 else epsb[:, 1:2]),
                     reads=["P:3", "epsb"], writes=["rstd"])
                S.op("dve", lambda e: e.reciprocal(out=rstd, in_=rstd), reads=["rstd"], writes=["rstd"])
                for c in range(nch):
                    S.op("dve", lambda e, c=c: e.tensor_tensor(out=dst[:, c, T * 512:(T + 1) * 512], in0=PB(c), in1=rstd,
                                                              op=ALU.mult), reads=["P:%d" % c, "rstd"], writes=[dst.name if False else "nrm"])

            for T in range(4):
                rms_group(T, 3, lambda c: (wA, "wA%d" % c, c * 128), cqn, 384.0 * 1e-6)
                rms_group(T, 2, lambda c: ((wA, "wA3", 384) if c == 0 else (wB, wBn, 0)), ckvn, 256.0 * 1e-6)
                mm_group([(PB(4)[0:96, :], wkr[:, k, :], xT[:, k, T * 512:(T + 1) * 512]) for k in range(8)],
                         ["wkr", "xT"], ["P:4"])
                mm_group([(PB(5)[0:96, :], wkrr[:, k, :], xT[:, k, T * 512:(T + 1) * 512]) for k in range(8)],
                         ["wkrr", "xT"], ["P:5"])
                S.op("dve", lambda e, T=T: e.tensor_tensor(out=tmpa[64:96, :], in0=PB(4)[64:96, :],
                                                          in1=cs[64:96, 0, T * 512:(T + 1) * 512], op=ALU.mult),
                     reads=["P:4", "cs"], writes=["tmpa"])
                S.op("dve", lambda e, T=T: e.tensor_tensor(out=tmpb[64:96, :], in0=PB(5)[64:96, :],
                                                          in1=cs[64:96, 1, T * 512:(T + 1) * 512], op=ALU.mult),
                     reads=["P:5", "cs"], writes=["tmpb"])
                S.op("dve", lambda e, T=T: e.tensor_tensor(out=krope[64:96, T * 512:(T + 1) * 512], in0=tmpa[64:96, :],
                                                          in1=tmpb[64:96, :], op=ALU.add),
                     reads=["tmpa", "tmpb"], writes=["krope"])

            if "d_cqn" in dbg:
                for c in range(3):
                    S.op("dve", lambda e, c=c: e.tensor_copy(out=stg[:, 0:512], in_=cqn[:, c, 0:512]), reads=["nrm"], writes=["stg"])
                    S.dma("sp", lambda e, c=c: e.dma_start(out=dbg["d_cqn"][c * 128:(c + 1) * 128, :], in_=stg[:, 0:512]), reads=["stg"])

            wZ = wcs[0]
            for c in range(4):
                load_piece(wZ, w_in[:, O_BZ + c * 128:O_BZ + (c + 1) * 128], 8, c * 128, 128, "wA%d" % c)

            S.op("pool", lambda e: e.memset(vaug, 2.0), writes=["vaug"])
            SC_B = 96.0 ** -0.5
            def mla_v(hg):
                for tt in range(16):
                    bk = 6 + tt % 2
                    mm_group([(PB(bk)[:, 0:256], ckvn[:, k, tt * 128:(tt + 1) * 128], wv[:, k, hg * 256:(hg + 1) * 256])
                              for k in range(2)], ["nrm", "wv"], ["P:%d" % bk])
                    for par in range(2):
                        src = PB(bk)[:, 0:256].rearrange("p (h f) -> p h f", h=4)[:, par::2, :]
                        dst = vaug[:, tt, par::2, par * 64:par * 64 + 64]
                        evac(dst, src, ["P:%d" % bk], ["vaug"])

            def prep_b(h):
                QT = QTs[h % 2]; KT = KTs[h % 2]
                qn = "QT%d" % (h % 2); kn = "KT%d" % (h % 2)
                par = h % 2
                szp = szs_b[(h // 2) % 2]; szn = "szb%d" % ((h // 2) % 2)
                for T in range(4):
                    mm_group([(PB(6)[0:96, :], wuq[:, k, h * 96:(h + 1) * 96], cqn[:, k, T * 512:(T + 1) * 512])
                              for k in range(3)], ["wuq", "nrm"], ["P:6"])
                    mm_group([(PB(7)[0:96, :], wuqr[:, k, h * 96:(h + 1) * 96], cqn[:, k, T * 512:(T + 1) * 512])
                              for k in range(3)], ["wuqr", "nrm"], ["P:7"])
                    evac(QT[0:64, T * 512:(T + 1) * 512], PB(6)[0:64, :], ["P:6"], [qn], eng="dve")
                    S.op("dve", lambda e: e.tensor_tensor(out=tmpa[64:96, :], in0=PB(6)[64:96, :],
                                                         in1=cs[64:96, 0, T * 512:(T + 1) * 512], op=ALU.mult),
                         reads=["P:6", "cs"], writes=["tmpa"])
                    S.op("dve", lambda e: e.tensor_tensor(out=tmpb[64:96, :], in0=PB(7)[64:96, :],
                                                         in1=cs[64:96, 1, T * 512:(T + 1) * 512], op=ALU.mult),
                         reads=["P:7", "cs"], writes=["tmpb"])
                    S.op("pool", lambda e: e.tensor_tensor(out=QT[64:96, T * 512:(T + 1) * 512],
                                                          in0=tmpa[64:96, :], in1=tmpb[64:96, :], op=ALU.add),
                         reads=["tmpa", "tmpb"], writes=[qn])
                    yield
                    mm_group([(PB(6)[0:64, :], wukv[:, k, h * 128:h * 128 + 64], ckvn[:, k, T * 512:(T + 1) * 512])
                              for k in range(2)], ["wukv", "nrm"], ["P:6"])
                    evac(KT[0:64, T * 512:(T + 1) * 512], PB(6)[0:64, :], ["P:6"], [kn], eng="act")
                    yield
                    if par == 0:
                        mm_group([(PB(7), wZ[:, k, (h // 2) * 128:(h // 2 + 1) * 128], xT[:, k, T * 512:(T + 1) * 512]) for k in range(8)],
                                 ["wA%d" % (h // 2), "xT"], ["P:7"])
                        S.op("act", lambda e: e.activation(out=tmpc, in_=PB(7), func=AF.Tanh, scale=0.5),
                             reads=["P:7"], writes=["tmpc"])
                        S.op("dve", lambda e: e.scalar_tensor_tensor(
                            out=szp[:, T * 512:(T + 1) * 512], in0=tmpc, scalar=1.0, in1=PB(7),
                            op0=ALU.add, op1=ALU.mult), reads=["tmpc", "P:7"], writes=[szn])
                        yield
                S.op("pool", lambda e: e.tensor_copy(out=KT[64:96, :], in_=krope[64:96, :]),
                     reads=["krope"], writes=[kn])
                yield

            def out_b(h):
                par = h % 2
                sz = szs_b[(h // 2) % 2]
                szn = "szb%d" % ((h // 2) % 2)

                def outfn(Q, abk):
                    obk = abk[0]
                    vr = slice(par * 64, par * 64 + 64)
                    sr = slice((1 - par) * 64, (1 - par) * 64 + 64)
                    r = rs[Q % 2]; rn = "rs%d" % (Q % 2)
                    S.op("dve", lambda e: e.reciprocal(out=r[vr, :], in_=PB(obk)[sr, :]), reads=["P:%d" % obk], writes=[rn])
                    S.op("pool", lambda e: e.tensor_tensor(out=r[vr, :], in0=r[vr, :], in1=sz[vr, Q * 512:(Q + 1) * 512], op=ALU.mult),
                         reads=[rn, szn], writes=[rn])
                    S.op("dve", lambda e: e.tensor_tensor(out=ybT[vr, h // 2, Q * 512:(Q + 1) * 512], in0=PB(obk)[vr, :],
                                                         in1=r[vr, :], op=ALU.mult), reads=["P:%d" % obk, rn], writes=["ybT"])
                return outfn

            for _ in prep_b(0):
                pass
            for h in range(8):
                if h % 4 == 0:
                    mla_v(h // 4)
                nxt = prep_b(h + 1) if h < 7 else None

                def Vfn(Q, kt, hh=h % 4):
                    return [(vaug[:, kt, hh, :], "vaug")]
                attention(("b", "QT%d" % (h % 2), "KT%d" % (h % 2)), QTs[h % 2], KTs[h % 2], (0, 96), Vfn, SC_B, range(4),
                          lambda Q: [4 + Q % 2], out_b(h), pts, sb=((0,), (1,), (2,), (3,)), fill=nxt, LA=3, grp=1, fill_every=2)
                if nxt is not None:
                    for _ in nxt:
                        pass

            def dump(name, srcT, nchunks, rname):
                if name in dbg:
                    for c in range(nchunks):
                        for T in range(4):
                            S.op("dve", lambda e, c=c, T=T: e.tensor_copy(out=dstg[:, 0:512], in_=srcT[:, c, T * 512:(T + 1) * 512]),
                                 reads=[rname], writes=["dstg"])
                            S.dma("sp", lambda e, c=c, T=T: e.dma_start(out=dbg[name][c * 128:(c + 1) * 128, T * 512:(T + 1) * 512],
                                                                       in_=dstg[:, 0:512]), reads=["dstg"])
            dstg = stg
            dump("d_yb", ybT, 4, "ybT")

            if UPTO <= 1:
                return
            S.barrier()
            o = PH
            vaug = V(o, BF16, [16, 4, 128]); o += 16 * KB
            QTs = [[V(o + (2 * i + j) * 4 * KB, BF16, [S_LEN]) for j in range(2)] for i in range(2)]; o += 16 * KB
            KTs = [[V(o + (2 * i + j) * 4 * KB, BF16, [S_LEN]) for j in range(2)] for i in range(2)]; o += 16 * KB
            pts = [V(o + i * 2 * KB, BF16, [1024]) for i in range(3)]
            pts6 = [V(o + i * KB, BF16, [512]) for i in range(6)]; o += 6 * KB
            wq4 = [V(o + i * 8 * KB, BF16, [8, 512]) for i in range(4)]; o += 32 * KB
            szs = [V(o + i * 4 * KB, BF16, [S_LEN]) for i in range(2)]; o += 8 * KB
            rs = [V(o + i * 2 * KB, F32, [512]) for i in range(2)]; o += 4 * KB
            tmpa = V(o, F32, [512]); o += 2 * KB
            tmpb = V(o, F32, [512]); o += 2 * KB
            tmpc = V(o, F32, [512]); o += 2 * KB
            cmpb = V(o - 4 * KB, F32, [1024])
            sqb = V(o, BF16, [512]); o += KB
            tmpas = [tmpa, V(o, F32, [512])]; o += 2 * KB
            tmpcs = [tmpc, V(o, F32, [512])]; o += 2 * KB
            sqbs = [sqb, V(o, BF16, [512])]; o += KB
            gt = V(o, F32, [128]); o += 512
            m8 = V(o, F32, [128]); o += 512
            msk = V(o, F32, [128]); o += 512
            mbb = V(o, BF16, [128]); o += 256
            ksum = V(o, F32, [16]); o += 64
            ksb = V(o, BF16, [16]); o += 32
            dstg = V(o, F32, [512]); o += 2 * KB
            lt = V(o, F32, [256]); o += KB
            lsm = V(o, F32, [8]); o += 32
            gs = V(o, F32, [2]); o += 32
            assert o <= 206 * KB, o
            KP = [(0, 80), (0, 80)]
            DR = [slice(0, 64), slice(0, 64)]
            PR = [slice(0, 64), slice(64, 128)]
            MR = [slice(64, 72), slice(64, 72)]
            AR = [slice(72, 80), slice(72, 80)]
            ZR = [slice(64, 96), slice(64, 96)]

            def tn(kind, i, j):
                return "%s%d%d" % (kind, i, j)

            wcs4 = wq4
            wc_rr[0] = 0
            wQ, wK, wV, wZ = wq4

            def load_pair(oq, ok, oz, pr):
                load_piece(wQ, w_in[:, oq + pr * 128:oq + (pr + 1) * 128], 8, pr * 128, 128, "w4q%d" % pr)
                load_piece(wK, w_in[:, ok + pr * 128:ok + (pr + 1) * 128], 8, pr * 128, 128, "w4k%d" % pr)
                load_piece(wZ, w_in[:, oz + pr * 128:oz + (pr + 1) * 128], 8, pr * 128, 128, "w4z%d" % pr)

            def load_v(ov, g):
                load_piece(wV, w_in[:, ov + g * 256:ov + (g + 1) * 256], 8, g * 256, 256, "w4v%d" % g)
            load_pair(O_AQ, O_AK, O_AZ, 0)
            for i in range(2):
                for j in range(2):
                    S.op("dve", lambda e: e.memset(QTs[i][j][ZR[j], :], 0.0), writes=[tn("QT", i, j)])
                    S.op("dve", lambda e: e.memset(KTs[i][j][ZR[j], :], 0.0), writes=[tn("KT", i, j)])
                    S.dma("pool", lambda e: e.dma_start(out=KTs[i][j][MR[j], :], in_=C["c_ind"]), writes=[tn("KT", i, j)])

            def alibi_a(pr):
                i = pr % 2
                for j in range(2):
                    h = 2 * pr + j
                    S.dma("pool", lambda e: e.dma_start(out=QTs[i][j][AR[j], :], in_=C["c_aq"][h]), writes=[tn("QT", i, j)])
                    S.dma("pool", lambda e: e.dma_start(out=KTs[i][j][AR[j], :], in_=C["c_ak"][h]), writes=[tn("KT", i, j)])
            alibi_a(0)
            load_v(O_AV, 0)
            load_pair(O_AQ, O_AK, O_AZ, 1)
            S.op("pool", lambda e: e.memset(vaug, 2.0), writes=["vaug"])

            def zgate(wbuf, wn, col, sz_, szn, T):
                mm_group([(PB(7), wbuf[:, k, col:col + 128], xT[:, k, T * 512:(T + 1) * 512]) for k in range(8)],
                         [wn, "xT"], ["P:7"])
                S.op("act", lambda e: e.activation(out=tmpa, in_=PB(7), func=AF.Tanh, scale=0.5),
                     reads=["P:7"], writes=["tmpa"])
                S.op("dve", lambda e: e.scalar_tensor_tensor(out=sz_[:, T * 512:(T + 1) * 512], in0=tmpa, scalar=1.0,
                                                            in1=PB(7), op0=ALU.add, op1=ALU.mult),
                     reads=["tmpa", "P:7"], writes=[szn])

            def v_tokmajor(src_sel, wbuf, wn, col0, ncols, dst_fn):
                for tt in range(16):
                    bk = 6 + tt % 2
                    mm_group([(PB(bk)[:, 0:ncols], xT[:, k, tt * 128:(tt + 1) * 128], wbuf[:, k, col0:col0 + ncols])
                              for k in range(8)], ["xT"] + list(wn), ["P:%d" % bk])
                    dst_fn(tt, bk)

            def norm_out(yT_, ynm, h, par, Q, obk, sz_, szn):
                vr = slice(par * 64, par * 64 + 64)
                sr = slice((1 - par) * 64, (1 - par) * 64 + 64)
                r = rs[Q % 2]; rn = "rs%d" % (Q % 2)
                S.op("dve", lambda e: e.reciprocal(out=r[vr, :], in_=PB(obk)[sr, :]), reads=["P:%d" % obk], writes=[rn])
                S.op("pool", lambda e: e.tensor_tensor(out=r[vr, :], in0=r[vr, :], in1=sz_[vr, Q * 512:(Q + 1) * 512], op=ALU.mult),
                     reads=[rn, szn], writes=[rn])
                S.op("dve", lambda e: e.tensor_tensor(out=yT_[vr, h // 2, Q * 512:(Q + 1) * 512], in0=PB(obk)[vr, :],
                                                     in1=r[vr, :], op=ALU.mult), reads=["P:%d" % obk, rn], writes=[ynm])

            def vdst(tt, bk):
                for par in range(2):
                    src = PB(bk)[:, 0:256].rearrange("p (h f) -> p h f", h=4)[:, par::2, :]
                    dst = vaug[:, tt, par::2, par * 64:par * 64 + 64]
                    evac(dst, src, ["P:%d" % bk], ["vaug"])

            def qkz_items(i, wqn, wkn, wzn, col, sz_, szn):
                items = []
                for T in range(4):
                    def pe_q(bk, T=T):
                        mm_group([(PB(bk), wQ[:, k, col:col + 128], xT[:, k, T * 512:(T + 1) * 512]) for k in range(8)],
                                 [wqn, "xT"], ["P:%d" % bk])

                    def post_q(bk, T=T):
                        for j in range(2):
                            evac(QTs[i][j][DR[j], T * 512:(T + 1) * 512], PB(bk)[PR[j], :], ["P:%d" % bk], [tn("QT", i, j)], eng="dve")

                    def pe_k(bk, T=T):
                        mm_group([(PB(bk), wK[:, k, col:col + 128], xT[:, k, T * 512:(T + 1) * 512]) for k in range(8)],
                                 [wkn, "xT"], ["P:%d" % bk])

                    def post_k(bk, T=T):
                        for j in range(2):
                            evac(KTs[i][j][DR[j], T * 512:(T + 1) * 512], PB(bk)[PR[j], :], ["P:%d" % bk], [tn("KT", i, j)], eng="act")

                    def pe_z(bk, T=T):
                        mm_group([(PB(bk), wZ[:, k, col:col + 128], xT[:, k, T * 512:(T + 1) * 512]) for k in range(8)],
                                 [wzn, "xT"], ["P:%d" % bk])

                    def post_z(bk, T=T):
                        S.op("act", lambda e: e.activation(out=tmpa, in_=PB(bk), func=AF.Tanh, scale=0.5),
                             reads=["P:%d" % bk], writes=["tmpa"])
                        S.op("dve", lambda e: e.scalar_tensor_tensor(out=sz_[:, T * 512:(T + 1) * 512], in0=tmpa, scalar=1.0,
                                                                    in1=PB(bk), op0=ALU.add, op1=ALU.mult),
                             reads=["tmpa", "P:%d" % bk], writes=[szn])
                    items += [(pe_q, post_q), (pe_k, post_k), (pe_z, post_z)]
                return items

            def skewed(items, fb=(6, 7)):
                prev = None
                for k, (pe, post) in enumerate(items):
                    bk = fb[k % len(fb)]
                    pe(bk)
                    if prev is not None:
                        prev[0](prev[1])
                    prev = (post, bk)
                    yield
                prev[0](prev[1])
                yield

            def prep_a(pr):
                i = pr % 2
                sz_ = szs[i]; szn = "sz%d" % i
                if pr >= 1:
                    alibi_a(pr)
                    if pr + 1 < 4:
                        load_pair(O_AQ, O_AK, O_AZ, pr + 1)
                    if pr == 1:
                        load_v(O_AV, 1)
                for _ in skewed(qkz_items(i, "w4q%d" % pr, "w4k%d" % pr, "w4z%d" % pr, pr * 128, sz_, szn)):
                    yield
                for j in range(2):
                    S.op("dve", lambda e: e.tensor_reduce(out=ksum[0:64, j * 8:(j + 1) * 8],
                                                         in_=KTs[i][j][0:64, :].rearrange("p (n k) -> p n k", n=8),
                                                         axis=AX.X, op=ALU.add), reads=[tn("KT", i, j)], writes=["ksum"])
                S.op("dve", lambda e: e.tensor_copy(out=ksb[0:64, :], in_=ksum[0:64, :]), reads=["ksum"], writes=["ksb"])
                for _ in range(10):
                    yield

                def gmm(e):
                    for j in range(2):
                        for qi in range(8):
                            e.matmul(PB(6)[:, j * 64 + qi * 8:j * 64 + (qi + 1) * 8], QTs[i][j][0:64, (8 + qi) * 128:(9 + qi) * 128],
                                     ksb[0:64, j * 8:(j + 1) * 8], start=True, stop=True)
                S.op("pe", gmm, reads=[tn("QT", i, 0), tn("QT", i, 1), "ksb"], writes=["P:6"])
                yield
                yield
                S.op("dve", lambda e: e.tensor_tensor(out=gt, in0=PB(6)[:, 0:128], in1=pen, op=ALU.add),
                     reads=["P:6", "pen"], writes=["gt"])
                yield
                g0 = gt[:, 0:1]
                pst = g0.ap[0][0]
                in_m = bass.AP(g0.tensor, g0.offset, [[pst, 128], [8, 16], [0, 8], [1, 8]])
                in_n = bass.AP(g0.tensor, g0.offset, [[pst, 128], [8, 16], [1, 8], [0, 8]])
                S.op("dve", lambda e: e.tensor_tensor(out=cmpb.rearrange("p (g n m) -> p g n m", g=16, n=8), in0=in_m, in1=in_n,
                                                     op=ALU.is_gt), reads=["gt"], writes=["cmpb"])
                yield
                S.op("dve", lambda e: e.tensor_reduce(out=m8, in_=cmpb.rearrange("p (a m) -> p a m", m=8), axis=AX.X, op=ALU.add),
                     reads=["cmpb"], writes=["m8"])
                yield
                S.op("dve", lambda e: e.tensor_scalar(out=msk, in0=m8, scalar1=2.5, scalar2=None, op0=ALU.is_lt),
                     reads=["m8"], writes=["msk"])
                yield
                S.op("dve", lambda e: e.tensor_tensor(out=msk, in0=msk, in1=own, op=ALU.max), reads=["msk", "own"], writes=["msk"])
                S.op("dve", lambda e: e.tensor_scalar(out=mbb, in0=msk, scalar1=1.0, scalar2=-NEGBIG, op0=ALU.subtract, op1=ALU.mult),
                     reads=["msk"], writes=["mbb"])
                for _ in range(6):
                    yield
                for j in range(2):
                    def gtr(e):
                        for qi in range(8):
                            e.transpose(out=PBb(7)[0:8, qi * 128:(qi + 1) * 128], in_=mbb[:, j * 64 + qi * 8:j * 64 + (qi + 1) * 8],
                                        identity=ident)
                    S.op("pe", gtr, reads=["mbb", "ident"], writes=["P:7"])
                    evac(QTs[i][j][MR[j], 1024:2048], PBb(7)[0:8, 0:1024], ["P:7"], [tn("QT", i, j)], eng="dve")
                    yield

            for _ in prep_a(0):
                pass
            for pr in range(4):
                i = pr % 2
                nxt = prep_a(pr + 1) if pr < 3 else None
                for j in range(2):
                    h = 2 * pr + j
                    if h % 4 == 0:
                        v_tokmajor(None, wV, ["w4v%d" % (h // 4)], (h // 4) * 256, 256, vdst)

                    def Vfn(Q, kt, hh=h % 4):
                        return [(vaug[:, kt, hh, :], "vaug")]
                    attention(("a", tn("QT", i, j), tn("KT", i, j)), QTs[i][j], KTs[i][j], KP[j], Vfn, 0.125, range(4),
                              lambda Q: [4 + Q % 2],
                              lambda Q, abk, h=h, j=j, i=i: norm_out(yaT, "yaT", h, j, Q, abk[0], szs[i], "sz%d" % i), pts6, sb=((0,), (1,), (2,), (3,)), fill=nxt, LA=3, grp=1, fill_every=2)
                if nxt is not None:
                    for _ in nxt:
                        pass
            dump("d_ya", yaT, 4, "yaT")

            if UPTO <= 2:
                return
            S.barrier()
            wc_rr[0] = 0
            def alibi_c(h):
                i = h % 2
                for j in range(2):
                    S.dma("pool", lambda e: e.dma_start(out=QTs[i][j][AR[j], :], in_=C["c_aq"][8 + h]), writes=[tn("QT", i, j)])
                    S.dma("pool", lambda e: e.dma_start(out=KTs[i][j][AR[j], :], in_=C["c_ak"][8 + h]), writes=[tn("KT", i, j)])

            for g in range(2):
                load_v(O_CV, g)
            load_pair(O_CQ, O_CK, O_CZ, 0)
            for i in range(2):
                for j in range(2):
                    S.op("dve", lambda e: e.memset(QTs[i][j][ZR[j], :], 0.0), writes=[tn("QT", i, j)])
            alibi_c(0)
            load_pair(O_CQ, O_CK, O_CZ, 1)
            S.dma("sp", lambda e: e.dma_start(out=lt, in_=W["diff_lambda"][li].rearrange("a b -> (a b)").partition_broadcast(128)),
                  writes=["lt"])
            S.op("dve", lambda e: e.tensor_tensor(out=lt[:, 0:64], in0=lt[:, 0:64], in1=lt[:, 64:128], op=ALU.mult), reads=["lt"], writes=["lt"])
            S.op("dve", lambda e: e.tensor_tensor(out=lt[:, 128:192], in0=lt[:, 128:192], in1=lt[:, 192:256], op=ALU.mult), reads=["lt"], writes=["lt"])
            S.op("dve", lambda e: e.tensor_reduce(out=lsm[:, 0:2], in_=lt.rearrange("p (a b) -> p a b", a=2)[:, :, 0:64], axis=AX.X, op=ALU.add),
                 reads=["lt"], writes=["lsm"])
            S.op("act", lambda e: e.activation(out=lsm[:, 2:4], in_=lsm[:, 0:2], func=AF.Exp), reads=["lsm"], writes=["lsm"])
            S.op("dve", lambda e: e.scalar_tensor_tensor(out=lsm[:, 4:5], in0=lsm[:, 3:4], scalar=float(-lam_init), in1=lsm[:, 2:3],
                                                        op0=ALU.add, op1=ALU.subtract), reads=["lsm"], writes=["lsm"])
            S.dma("sp", lambda e: e.dma_start(out=gs[:, 0:1], in_=W["diff_subln_g"][li].rearrange("(p a) -> p a", a=1)), writes=["gs"])
            S.op("dve", lambda e: e.tensor_scalar(out=gs[:, 0:1], in0=gs[:, 0:1], scalar1=float(0.5 * math.sqrt(128.0) * (1.0 - lam_init)),
                                                 scalar2=None, op0=ALU.mult), reads=["gs"], writes=["gs"])
            vd = vaug.rearrange("p a b c -> p a (b c)")

            def vdst2(tt, bk):
                evac(vd[:, tt, :], PB(bk), ["P:%d" % bk], ["vaug"])
            v_tokmajor(None, wV, ["w4v0", "w4v1"], 0, 512, vdst2)

            def prep_c(h):
                i = h % 2
                if h + 1 < 4:
                    alibi_c(h + 1)
                if 1 <= h and h + 1 < 4:
                    load_pair(O_CQ, O_CK, O_CZ, h + 1)
                for _ in skewed(qkz_items(i, "w4q%d" % h, "w4k%d" % h, "w4z%d" % h, h * 128, szs[i], "sz%d" % i), fb=(0, 1, 2)):
                    yield

            dq = collections.deque()

            def filler(g):
                while True:
                    if dq:
                        f = dq.popleft()
                        if f is not None:
                            f()
                    if g is not None:
                        next(g, None)
                    yield

            for h in range(4):
                i = h % 2
                sz_ = szs[i]; szn = "sz%d" % i
                for _ in prep_c(h):
                    pass
                nxt = None

                def Vfn(ji, kt, h=h):
                    return [(vd[:, kt, h * 128:(h + 1) * 128], "vaug"), (ones, "ones")]

                def comb(ji, abk, h=h, sz_=sz_, szn=szn):
                    if ji % 2 == 0:
                        return
                    Q = ji // 2
                    ta = tmpas[Q % 2]; tan = "tmpa%d" % (Q % 2)
                    tc = tmpcs[Q % 2]; tcn = "tmpc%d" % (Q % 2)
                    sqq = sqbs[Q % 2]; sqn = "sqb%d" % (Q % 2)
                    S.op("dve", lambda e: e.reciprocal(out=ta, in_=PB(6)), reads=["P:6"], writes=[tan])
                    S.op("dve", lambda e: e.tensor_tensor(out=ta, in0=PB(4), in1=ta, op=ALU.mult), reads=["P:4", tan], writes=[tan])
                    S.op("dve", lambda e: e.reciprocal(out=tmpb, in_=PB(7)), reads=["P:7"], writes=["tmpb"])
                    S.op("dve", lambda e: e.tensor_tensor(out=tmpb, in0=PB(5), in1=tmpb, op=ALU.mult), reads=["P:5", "tmpb"], writes=["tmpb"])
                    S.op("dve", lambda e: e.scalar_tensor_tensor(out=ta, in0=tmpb, scalar=lsm[:, 4:5], in1=ta, op0=ALU.mult, op1=ALU.add),
                         reads=[tan, "tmpb", "lsm"], writes=[tan])
                    S.op("dve", lambda e: e.tensor_tensor(out=sqq, in0=ta, in1=ta, op=ALU.mult), reads=[tan], writes=[sqn])

                    def stage2():
                        mm_group([(PB(3), ones, sqq)], ["ones", sqn], ["P:3"])
                        S.op("act", lambda e: e.activation(out=tc, in_=PB(3), func=AF.Sqrt, bias=epsb[:, 2:3]), reads=["P:3", "epsb"], writes=[tcn])
                        S.op("dve", lambda e: e.reciprocal(out=tc, in_=tc), reads=[tcn], writes=[tcn])
                        S.op("dve", lambda e: e.tensor_tensor(out=tc, in0=tc, in1=sz_[:, Q * 512:(Q + 1) * 512], op=ALU.mult),
                             reads=[tcn, szn], writes=[tcn])
                        S.op("dve", lambda e: e.scalar_tensor_tensor(out=ycT[:, h, Q * 512:(Q + 1) * 512], in0=ta, scalar=gs[:, 0:1],
                                                                    in1=tc, op0=ALU.mult, op1=ALU.mult),
                             reads=[tan, tcn, "gs"], writes=["ycT"])
                    dq.extend([None] * 8 + [stage2])
                nm = [("c", QTs[i][ji % 2], KTs[i][ji % 2], tn("QT", i, ji % 2), tn("KT", i, ji % 2), KP[ji % 2]) for ji in range(8)]
                attention(nm, None, None, None, Vfn, 0.125, [ji // 2 for ji in range(8)],
                          lambda ji: [4 + ji % 2, 6 + ji % 2], comb, pts, sb=((0,), (1,), (2,)), fill=filler(nxt), LA=2, grp=1)
                while dq:
                    f = dq.popleft()
                    if f is not None:
                        f()
                if nxt is not None:
                    for _ in nxt:
                        pass
            dump("d_yc", ycT, 4, "ycT")

            if UPTO <= 3:
                return
            S.barrier()
            o = PH
            mergedT = V(o, BF16, [8, S_LEN]); o += 32 * KB
            o_tail = o
            wbr = [V(o + i * 8 * KB, BF16, [4, 1024]) for i in range(3)]; o += 24 * KB
            wmj = [V(o + i * 6 * KB, BF16, [8, 384]) for i in range(2)]; o += 12 * KB
            tts = [V(o + i * 2 * KB, F32, [512]) for i in range(3)]; o += 6 * KB
            mts = [V(o + i * 2 * KB, F32, [512]) for i in range(3)]; o += 6 * KB
            dstg = V(o, F32, [512]); o += 2 * KB
            assert o <= PH + 84 * KB, o
            w_o = V(PH + 84 * KB, BF16, [8, 1024])
            w_pg = V(PH + 100 * KB, BF16, [8, 1024])
            yTs = [(yaT, "yaT"), (ybT, "ybT"), (ycT, "ycT")]
            WBR = ["w_branch_a", "w_branch_b", "w_branch_c"]

            def merge_loads(j):
                wm = wmj[j % 2]; wmn = "wmj%d" % (j % 2)
                for br in range(3):
                    load_piece(wbr[br], W[WBR[br]][li][:, j * 128:(j + 1) * 128], 4, j * 128, 128, "wbr%d_%d" % (br, j))
                    S.dma("pool", lambda e: e.dma_start(
                        out=wm[:, :, br * 128:(br + 1) * 128],
                        in_=W["w_merge"][li][:, br * 1024 + j * 128:br * 1024 + (j + 1) * 128].rearrange("(c p) n -> p c n", p=128)),
                        writes=[wmn])

            merge_loads(0)
            for j in range(8):
                if j + 1 < 8:
                    merge_loads(j + 1)
                if j == 6:
                    S.dma("pool", lambda e: e.dma_start(out=w_o, in_=W["w_out"][li].rearrange("(c p) n -> p c n", p=128)), writes=["w_o"])
                    S.dma("pool", lambda e: e.dma_start(out=w_pg, in_=W["w_ple_gate"][li].rearrange("(c p) n -> p c n", p=128)), writes=["w_pg"])
                wm = wmj[j % 2]; wmn = "wmj%d" % (j % 2)
                for T in range(4):
                    for br in range(3):
                        yT_, ynm = yTs[br]
                        mm_group([(PB(br), wbr[br][:, k, j * 128:(j + 1) * 128], yT_[:, k, T * 512:(T + 1) * 512]) for k in range(4)],
                                 ["wbr%d_%d" % (br, j), ynm], ["P:%d" % br])
                        mm_group([(PB(3 + br), wm[:, k, br * 128:(br + 1) * 128], xT[:, k, T * 512:(T + 1) * 512]) for k in range(8)],
                                 [wmn, "xT"], ["P:%d" % (3 + br)])
                        S.op("act", lambda e: e.activation(out=tts[br], in_=PB(3 + br), func=AF.Tanh, scale=0.5,
                                                           bias=hb[:, br * 8 + j:br * 8 + j + 1]),
                             reads=["P:%d" % (3 + br), "hb"], writes=["tt%d" % br])
                        S.op("dve", lambda e: e.scalar_tensor_tensor(out=mts[br], in0=tts[br], scalar=1.0, in1=PB(br),
                                                                    op0=ALU.add, op1=ALU.mult),
                             reads=["tt%d" % br, "P:%d" % br], writes=["mt%d" % br])
                    S.op("dve", lambda e: e.tensor_tensor(out=mts[0], in0=mts[0], in1=mts[1], op=ALU.add), reads=["mt0", "mt1"], writes=["mt0"])
                    S.op("dve", lambda e: e.tensor_tensor(out=mergedT[:, j, T * 512:(T + 1) * 512], in0=mts[0], in1=mts[2], op=ALU.add),
                         reads=["mt0", "mt2"], writes=["mergedT"])
            dump("d_mg", mergedT, 8, "mergedT")

            if UPTO <= 4:
                return
            S.barrier()
            o = o_tail
            w_p = V(o, BF16, [2, 1024]); o += 4 * KB
            lng = V(o, F32, [1024]); o += 4 * KB
            lnb = V(o, F32, [1024]); o += 4 * KB
            xts = [V(o + i * 4 * KB, F32, [1024]) for i in range(2)]; o += 8 * KB
            rps = [V(40 * KB + i * 4 * KB, F32, [1024]) for i in range(5)]
            rbs = [V(o + i * 2 * KB, BF16, [1024]) for i in range(2)]; o += 4 * KB
            rTs = [V(o + i * 2 * KB, BF16, [8, 128]) for i in range(2)]; o += 4 * KB
            ths = [V(o + i * 2 * KB, F32, [512]) for i in range(2)]; o += 4 * KB
            pps = [V(o + i * 2 * KB, F32, [512]) for i in range(2)]; o += 4 * KB
            ys = [V(o + i * 4 * KB, F32, [1024]) for i in range(3)]; o += 12 * KB
            st8s = [V(o + i * 32, F32, [8]) for i in range(3)]; o += 96
            assert o <= PH + 84 * KB, o
            S.dma("pool", lambda e: e.dma_start(out=w_p, in_=W["w_ple"][li].rearrange("(c p) n -> p c n", p=128)), writes=["w_p"])
            S.dma("sp", lambda e: e.dma_start(out=lng, in_=W["ln_g"][li].partition_broadcast(128)), writes=["lng"])
            S.dma("sp", lambda e: e.dma_start(out=lnb, in_=W["ln_b"][li].partition_broadcast(128)), writes=["lnb"])
            C1 = 0.5 / ALPHA
            def xload(tt):
                xdep = ["Y%d" % tt] if xsrc is not x_in else []
                S.dma("sp", lambda e: e.dma_start(out=xts[tt % 2], in_=xsrc[tt * 128:(tt + 1) * 128, :]), reads=xdep,
                      writes=["xt%d" % (tt % 2)])

            def stageA(tt):
                xt = xts[tt % 2]; xn = "xt%d" % (tt % 2)
                rp = rps[tt % 5]; rpn = "rp%d" % (tt % 5)
                rb = rbs[tt % 2]; rbn = "rb%d" % (tt % 2)
                for hf in range(2):
                    mm_group([(PB(hf), mergedT[:, c, tt * 128:(tt + 1) * 128], w_o[:, c, hf * 512:(hf + 1) * 512]) for c in range(8)],
                             ["mergedT", "w_o"], ["P:%d" % hf])
                    yield
                    S.op("dve", lambda e: e.scalar_tensor_tensor(
                        out=rp[:, hf * 512:(hf + 1) * 512], in0=PB(hf), scalar=float(C1), in1=xt[:, hf * 512:(hf + 1) * 512],
                        op0=ALU.mult, op1=ALU.add), reads=["P:%d" % hf, xn], writes=[rpn])
                    yield
                S.op("dve", lambda e: e.tensor_copy(out=rb, in_=rp), reads=[rpn], writes=[rbn])
                yield

            def stageB1(tt):
                rb = rbs[tt % 2]; rbn = "rb%d" % (tt % 2)
                rT = rTs[tt % 2]; rTn = "rT%d" % (tt % 2)

                def trr(e):
                    for c in range(8):
                        e.transpose(out=PBb(2)[:, c * 128:(c + 1) * 128], in_=rb[:, c * 128:(c + 1) * 128], identity=ident)
                S.op("pe", trr, reads=[rbn, "ident"], writes=["P:2"])
                yield
                evac(rT, PBb(2).rearrange("p (c t) -> p c t", c=8), ["P:2"], [rTn], eng="act")
                yield

            def stageB2(tt):
                st8b = st8s[tt % 3]; stnb = "st8%d" % (tt % 3)
                rp = rps[tt % 5]; rpn = "rp%d" % (tt % 5)
                rT = rTs[tt % 2]; rTn = "rT%d" % (tt % 2)
                for hf in range(2):
                    mm_group([(PB(5 + hf), pT[:, c, tt * 128:(tt + 1) * 128], w_p[:, c, hf * 512:(hf + 1) * 512]) for c in range(2)],
                             ["pT", "w_p"], ["P:%d" % (5 + hf)])
                    yield
                    S.op("act", lambda e: e.activation(out=pps[hf], in_=PB(5 + hf), func=AF.Copy), reads=["P:%d" % (5 + hf)],
                         writes=["pp%d" % hf])
                    yield
                for hf in range(2):
                    mm_group([(PB(3 + hf), rT[:, c, :], w_pg[:, c, hf * 512:(hf + 1) * 512]) for c in range(8)],
                             [rTn, "w_pg"], ["P:%d" % (3 + hf)])
                    yield
                    S.op("act", lambda e: e.activation(out=ths[hf], in_=PB(3 + hf), func=AF.Tanh, scale=float(0.5 * ALPHA)),
                         reads=["P:%d" % (3 + hf)], writes=["th%d" % hf])
                    yield
                for hf in range(2):
                    S.op("dve", lambda e: e.scalar_tensor_tensor(out=ths[hf], in0=ths[hf], scalar=1.0, in1=pps[hf],
                                                                op0=ALU.add, op1=ALU.mult),
                         reads=["th%d" % hf, "pp%d" % hf], writes=["th%d" % hf])
                    yield
                    S.op("dve", lambda e: e.scalar_tensor_tensor(
                        out=rp[:, hf * 512:(hf + 1) * 512], in0=ths[hf], scalar=float(C1), in1=rp[:, hf * 512:(hf + 1) * 512],
                        op0=ALU.mult, op1=ALU.add, accum_out=st8b[:, 7 * hf:7 * hf + 1]), reads=["th%d" % hf, rpn], writes=[rpn, stnb])
                    yield

            def stageC1(tt):
                rp = rps[tt % 5]; rpn = "rp%d" % (tt % 5)
                y = ys[tt % 3]; yn = "y%d" % (tt % 3)
                st8 = st8s[tt % 3]; stn = "st8%d" % (tt % 3)
                S.op("act", lambda e: e.activation(out=y, in_=rp, func=AF.Square, accum_out=st8[:, 1:2]), reads=[rpn], writes=[yn, stn])
                yield
                S.op("dve", lambda e: e.tensor_tensor(out=st8[:, 0:1], in0=st8[:, 0:1], in1=st8[:, 7:8], op=ALU.add), reads=[stn], writes=[stn])
                yield
                S.op("dve", lambda e: e.tensor_scalar(out=st8[:, 2:3], in0=st8[:, 0:1], scalar1=float(-1.0 / D), scalar2=None, op0=ALU.mult),
                     reads=[stn], writes=[stn])
                yield
                S.op("dve", lambda e: e.tensor_tensor(out=st8[:, 3:4], in0=st8[:, 2:3], in1=st8[:, 2:3], op=ALU.mult), reads=[stn], writes=[stn])
                yield
                S.op("dve", lambda e: e.scalar_tensor_tensor(out=st8[:, 4:5], in0=st8[:, 1:2], scalar=float(1.0 / D), in1=st8[:, 3:4],
                                                            op0=ALU.mult, op1=ALU.subtract), reads=[stn], writes=[stn])
                yield

            def stageC2(tt):
                rp = rps[tt % 5]; rpn = "rp%d" % (tt % 5)
                y = ys[tt % 3]; yn = "y%d" % (tt % 3)
                st8 = st8s[tt % 3]; stn = "st8%d" % (tt % 3)
                S.op("act", lambda e: e.activation(out=st8[:, 5:6], in_=st8[:, 4:5], func=AF.Sqrt, bias=epsb[:, 3:4]), reads=[stn, "epsb"], writes=[stn])
                yield
                yield
                S.op("dve", lambda e: e.reciprocal(out=st8[:, 6:7], in_=st8[:, 5:6]), reads=[stn], writes=[stn])
                yield
                S.op("dve", lambda e: e.tensor_scalar(out=y, in0=rp, scalar1=st8[:, 2:3], scalar2=st8[:, 6:7],
                                                     op0=ALU.add, op1=ALU.mult), reads=[rpn, stn], writes=[yn])
                yield
                S.op("pool", lambda e: e.tensor_tensor(out=y, in0=y, in1=lng, op=ALU.mult), reads=[yn, "lng"], writes=[yn])
                yield
                S.op("pool", lambda e: e.tensor_tensor(out=y, in0=y, in1=lnb, op=ALU.add), reads=[yn, "lnb"], writes=[yn])
                yield
                S.dma("sp", lambda e: e.dma_start(out=ydst[tt * 128:(tt + 1) * 128, :], in_=y), reads=[yn],
                      writes=(["Y%d" % tt] if ydst is not out else []))
                yield

            def rr(*gens):
                gens = [g for g in gens if g is not None]
                while gens:
                    for g in list(gens):
                        try:
                            next(g)
                        except StopIteration:
                            gens.remove(g)

            xload(0)
            xload(1)
            rr(stageA(0))
            for tt in range(19):
                rr(stageC2(tt - 3) if 3 <= tt < 19 else None,
                   stageC1(tt - 2) if 2 <= tt < 18 else None,
                   stageB2(tt - 1) if 1 <= tt < 17 else None,
                   stageA(tt + 1) if tt + 1 < 16 else None,
                   stageB1(tt) if tt < 16 else None)
                if tt + 2 < 16:
                    xload(tt + 2)

        load_consts()
        cur = x_in
        for i, li in enumerate(layers):
            dst = out if i == len(layers) - 1 else scr
            layer(li, cur, dst)
            cur = dst
        S.emit(nc)
    return nc


FUSED = True


def _in_maps(x, p, inputs, consts):
    maps = []
    shared = {n: np.ascontiguousarray(np.asarray(inputs[n], dtype=np.float32)) for n in WNAMES}
    for b in range(NCORES):
        m = {"x": np.ascontiguousarray(x[b]), "p": np.ascontiguousarray(p[:, b])}
        m.update(shared)
        m.update(consts)
        maps.append(m)
    return maps


def kernel(**inputs):
    consts = make_consts()
    x = np.asarray(inputs["x"], dtype=np.float32)
    p = np.asarray(inputs["p"], dtype=np.float32)
    if FUSED:
        nc = build_program([0, 1])
        res = run_bass_kernel_spmd(nc, _in_maps(x, p, inputs, consts), core_ids=list(range(NCORES)))
        return np.stack([np.asarray(r["out"], dtype=np.float32) for r in res.results], axis=0)
    cur = x
    for li in range(DEPTH):
        nc = build_program([li])
        res = run_bass_kernel_spmd(nc, _in_maps(cur, p, inputs, consts), core_ids=list(range(NCORES)))
        cur = np.stack([np.asarray(r["out"], dtype=np.float32) for r in res.results], axis=0)
    return cur
```

```python
import collections
import contextlib
import math
import numpy as np
import ml_dtypes
import concourse.bass as bass
import concourse.mybir as mybir
from concourse.bass_utils import run_bass_kernel_spmd

F32 = mybir.dt.float32
BF16 = mybir.dt.bfloat16
U8 = mybir.dt.uint8
AF = mybir.ActivationFunctionType
ALU = mybir.AluOpType
AX = mybir.AxisListType

S_LEN = 2048
D = 1024
DEPTH = 2
NCORES = 8
IN_W = 5280
KB = 1024
ALPHA = (2 * DEPTH) ** 0.25
NEGBIG = -30000.0
UPTO = 5

O_AQ, O_AK, O_AV, O_AZ = 0, 512, 1024, 1536
O_BCQ, O_BCKV, O_BKR, O_BZ = 2048, 2432, 2688, 2720
O_CQ, O_CK, O_CV, O_CZ = 3232, 3744, 4256, 4768

ENGS = ("pe", "act", "dve", "pool", "sp")
N_DMA_SEMS = 24


class _Rec:
    def __init__(self):
        self.calls = []

    def __getattr__(self, name):
        def f(*a, **k):
            self.calls.append((name, a, k))
            return self
        return f


def _freeze(fn):
    rec = _Rec()
    fn(rec)
    calls = rec.calls

    def replay(engine):
        ins = None
        for (name, a, k) in calls:
            ins = getattr(engine, name)(*a, **k)
        return ins
    return replay


class Sched:
    def __init__(self):
        self.q = {e: [] for e in ENGS}
        self.cnt = {e: 0 for e in ENGS}
        self.seen = {e: {} for e in ENGS}
        self.last_w = {}
        self.readers = {}
        self.dma_cnt = [0] * N_DMA_SEMS
        self.dma_rr = 0
        self.pending = {e: [] for e in ENGS}

    @staticmethod
    def _split(reads, writes):
        excl = [b for b in reads if b.startswith("P:")]
        reads = [b for b in reads if not b.startswith("P:")]
        return reads, list(writes) + excl

    def _deps(self, reads, writes):
        reads, writes = self._split(reads, writes)
        deps = []
        for b in reads:
            w = self.last_w.get(b)
            if w is not None:
                deps.append(w)
        for b in writes:
            w = self.last_w.get(b)
            if w is not None:
                deps.append(w)
            deps.extend(self.readers.get(b, ()))
        return deps

    def _waits(self, eng, deps):
        need = {}
        for (k, v) in deps:
            if v > need.get(k, 0):
                need[k] = v
        out = []
        for k, v in need.items():
            if self.seen[eng].get(k, 0) < v:
                self.seen[eng][k] = v
                out.append((k, v))
        return out

    def _commit(self, reads, writes, tok):
        reads, writes = self._split(reads, writes)
        for b in reads:
            self.readers.setdefault(b, []).append(tok)
        for b in writes:
            self.last_w[b] = tok
            self.readers[b] = []

    def barrier(self):
        toks = [(e, self.cnt[e]) for e in ENGS if self.cnt[e] > 0]
        toks += [(("dma", i), 16 * self.dma_cnt[i]) for i in range(N_DMA_SEMS) if self.dma_cnt[i] > 0]
        for e in ENGS:
            self.pending[e] = list(toks)

    def op(self, eng, fn, reads=(), writes=()):
        deps = self._deps(reads, writes) + self.pending[eng]
        self.pending[eng] = []
        if eng == "pe":
            deps = [d for d in deps if d[0] != "pe"]
        waits = self._waits(eng, deps)
        self.cnt[eng] += 1
        tok = (eng, self.cnt[eng])
        self.q[eng].append((waits, _freeze(fn), (eng, 1)))
        self._commit(reads, writes, tok)
        return tok

    def dma(self, eng, fn, reads=(), writes=()):
        s = self.dma_rr
        self.dma_rr = (self.dma_rr + 1) % N_DMA_SEMS
        deps = self._deps(reads, writes) + self.pending[eng]
        self.pending[eng] = []
        key = ("dma", s)
        if self.dma_cnt[s] > 0:
            deps.append((key, 16 * self.dma_cnt[s]))
        waits = self._waits(eng, deps)
        self.dma_cnt[s] += 1
        tok = (key, 16 * self.dma_cnt[s])
        self.q[eng].append((waits, _freeze(fn), (key, 16)))
        self._commit(reads, writes, tok)
        return tok

    def emit(self, nc, final_eng="sp"):
        with contextlib.ExitStack() as st:
            sems = {}
            for e in ENGS:
                sems[e] = st.enter_context(nc.semaphore("s_" + e))
            for i in range(N_DMA_SEMS):
                sems[("dma", i)] = st.enter_context(nc.semaphore("s_dma%d" % i))
            fin = [(e, self.cnt[e]) for e in ENGS if self.cnt[e] > 0]
            fin += [(("dma", i), 16 * self.dma_cnt[i]) for i in range(N_DMA_SEMS) if self.dma_cnt[i] > 0]
            block = st.enter_context(nc.Block())
            handles = {"pe": block.tensor, "act": block.scalar, "dve": block.vector,
                       "pool": block.gpsimd, "sp": block.sync}
            for e in ENGS:
                ops = self.q[e]
                is_final = (e == final_eng)
                if not ops and not is_final:
                    continue

                def body(engine, ops=ops, is_final=is_final):
                    for (waits, fn, inc) in ops:
                        for (k, v) in waits:
                            engine.wait_ge(sems[k], v)
                        ins = fn(engine)
                        ins.then_inc(sems[inc[0]], inc[1])
                    if is_final:
                        for (k, v) in fin:
                            engine.wait_ge(sems[k], v)
                handles[e](body)


def _bf16_split3(v):
    v = v.astype(np.float32)
    hi = v.astype(ml_dtypes.bfloat16).astype(np.float32)
    r = v - hi
    mid = r.astype(ml_dtypes.bfloat16).astype(np.float32)
    lo = (r - mid).astype(ml_dtypes.bfloat16).astype(np.float32)
    return hi, mid, lo


def make_consts():
    n = 12
    s = 2.0 ** (-8.0 * (np.arange(n) + 1) / n)
    diff_idx = np.arange(4) * 3
    moba_idx = np.setdiff1d(np.arange(n), diff_idx)
    slopes = np.concatenate([s[moba_idx], s[diff_idx]]).astype(np.float32)
    pos = np.arange(S_LEN, dtype=np.float32)
    scale = 0.125
    aq = np.zeros((12, 8, S_LEN), np.float32)
    ak = np.zeros((12, 8, S_LEN), np.float32)
    for h in range(12):
        c = np.float32(slopes[h]) / np.float32(scale)
        v = (c * pos).astype(np.float32)
        hi, mid, lo = _bf16_split3(-v)
        aq[h, 0], aq[h, 1], aq[h, 2] = hi, mid, lo
        aq[h, 3:6] = 1.0
        hi, mid, lo = _bf16_split3(v)
        ak[h, 0:3] = 1.0
        ak[h, 3], ak[h, 4], ak[h, 5] = hi, mid, lo
    ind = np.zeros((8, S_LEN), np.float32)
    for b in range(8):
        ind[b, b * 256:(b + 1) * 256] = 1.0
    ident = np.eye(128, dtype=np.float32)
    kk = np.arange(128)[:, None]
    qq = np.arange(128)[None, :]
    tri = np.where(kk > qq, NEGBIG, 0.0).astype(np.float32)
    freqs = (10000.0 ** (-np.arange(0, 32, 2, dtype=np.float32) / 32)).astype(np.float32)
    ang = pos[None, :] * freqs[:, None]
    cs = np.zeros((2, 32, S_LEN), np.float32)
    cs[0] = np.concatenate([np.cos(ang), np.cos(ang)], axis=0)
    cs[1] = np.concatenate([np.sin(ang), np.sin(ang)], axis=0)
    pen = np.zeros((128, 8, 8), np.float32)
    own = np.zeros((128, 8, 8), np.float32)
    for qi in range(8):
        qb = (8 + qi) // 2
        for b in range(8):
            pen[:, qi, b] = 0.0 if b < qb else -1e30
            own[:, qi, b] = 1.0 if b >= qb else 0.0
    pen2 = np.concatenate([pen.reshape(128, 64)] * 2, axis=1)
    own2 = np.concatenate([own.reshape(128, 64)] * 2, axis=1)
    return dict(c_aq=aq, c_ak=ak, c_ind=ind, c_ident=ident, c_tri=tri, c_cs=cs.astype(np.float32),
                c_pen=np.ascontiguousarray(pen2), c_own=np.ascontiguousarray(own2))


WNAMES = ["w_in", "mla_q_norm_g", "mla_kv_norm_g", "mla_w_uq", "mla_w_ukv", "diff_lambda", "diff_subln_g",
          "w_branch_a", "w_branch_b", "w_branch_c", "w_merge", "b_merge", "w_out", "ln_g", "ln_b",
          "w_ple_gate", "w_ple"]
WSHAPES = {"w_in": [DEPTH, D, IN_W], "mla_q_norm_g": [DEPTH, 384], "mla_kv_norm_g": [DEPTH, 256],
           "mla_w_uq": [DEPTH, 384, 768], "mla_w_ukv": [DEPTH, 256, 1024], "diff_lambda": [DEPTH, 4, 64],
           "diff_subln_g": [DEPTH, 128], "w_branch_a": [DEPTH, 512, D], "w_branch_b": [DEPTH, 512, D],
           "w_branch_c": [DEPTH, 512, D], "w_merge": [DEPTH, D, 3 * D], "b_merge": [DEPTH, 3 * D],
           "w_out": [DEPTH, D, D], "ln_g": [DEPTH, D], "ln_b": [DEPTH, D], "w_ple_gate": [DEPTH, D, D],
           "w_ple": [DEPTH, 256, D]}
CSHAPES = {"c_aq": [12, 8, S_LEN], "c_ak": [12, 8, S_LEN], "c_ind": [8, S_LEN], "c_ident": [128, 128],
           "c_tri": [128, 128], "c_cs": [2, 32, S_LEN], "c_pen": [128, 128], "c_own": [128, 128]}


def build_program(layers, debug=()):
    nc = bass.Bass("TRN2", target_bir_lowering=False)
    x_in = nc.dram_tensor("x", [S_LEN, D], F32, kind="ExternalInput").ap()
    p_in = nc.dram_tensor("p", [DEPTH, S_LEN, 256], F32, kind="ExternalInput").ap()
    W = {n: nc.dram_tensor(n, WSHAPES[n], F32, kind="ExternalInput").ap() for n in WNAMES}
    C = {n: nc.dram_tensor(n, CSHAPES[n], F32, kind="ExternalInput").ap() for n in CSHAPES}
    out = nc.dram_tensor("out", [S_LEN, D], F32, kind="ExternalOutput").ap()
    scr = None
    if len(layers) > 1:
        scr = nc.dram_tensor("scr", [S_LEN, D], F32, kind="Internal").ap()
    dbg = {}
    for (name, shape) in debug:
        dbg[name] = nc.dram_tensor(name, shape, F32, kind="ExternalOutput").ap()

    S = Sched()
    with contextlib.ExitStack() as st:
        arena = st.enter_context(nc.sbuf_tensor("arena", [128, 206 * KB], U8))
        psall = st.enter_context(nc.psum_tensor("psall", [128, 4096], F32))

        def V(off, dtype, shape):
            es = 2 if dtype == BF16 else 4
            n = int(np.prod(shape))
            assert off + n * es <= 206 * KB, (off, n * es)
            ap = arena[:, off:off + n * es].bitcast(dtype)
            if len(shape) == 2:
                ap = ap.rearrange("p (a b) -> p a b", a=shape[0], b=shape[1])
            elif len(shape) == 3:
                ap = ap.rearrange("p (a b c) -> p a b c", a=shape[0], b=shape[1], c=shape[2])
            return ap

        def PB(i):
            return psall[:, i * 512:(i + 1) * 512]

        def PBb(i):
            return psall[:, i * 512:(i + 1) * 512].bitcast(BF16)

        xT = V(0, BF16, [8, S_LEN])
        pT = V(32 * KB, BF16, [2, S_LEN])
        ybT = V(40 * KB, BF16, [4, S_LEN])
        yaT = V(56 * KB, BF16, [4, S_LEN])
        ycT = V(72 * KB, BF16, [4, S_LEN])
        CO = 88 * KB
        ident = V(CO, BF16, [128])
        tri = V(CO + 256, BF16, [128])
        ones = V(CO + 512, BF16, [128])
        pen = V(CO + 768, F32, [128])
        own = V(CO + 1280, F32, [128])
        epsb = V(CO + 1792, F32, [8])
        hb = V(CO + 1824, F32, [24])
        PH = 90 * KB

        cp_rr = [0]

        def evac(out_ap, in_ap, reads, writes, eng=None):
            if eng is None:
                eng = ("act", "dve")[cp_rr[0] % 2]
                cp_rr[0] += 1
            if eng == "act":
                S.op("act", lambda e: e.activation(out=out_ap, in_=in_ap, func=AF.Copy), reads=reads, writes=writes)
            else:
                S.op("dve", lambda e: e.tensor_copy(out=out_ap, in_=in_ap), reads=reads, writes=writes)

        def mm_group(mms, reads, writes):
            def fn(e, mms=mms):
                ins = None
                n = len(mms)
                for i, (o, l, r) in enumerate(mms):
                    ins = e.matmul(o, l, r, start=(i == 0), stop=(i == n - 1))
                return ins
            S.op("pe", fn, reads=reads, writes=writes)

        def load_consts():
            S.dma("pool", lambda e: e.dma_start(out=ident, in_=C["c_ident"]), writes=["ident"])
            S.dma("pool", lambda e: e.dma_start(out=tri, in_=C["c_tri"]), writes=["tri"])
            S.op("pool", lambda e: e.memset(ones, 1.0), writes=["ones"])
            S.dma("sp", lambda e: e.dma_start(out=pen, in_=C["c_pen"]), writes=["pen"])
            S.dma("sp", lambda e: e.dma_start(out=own, in_=C["c_own"]), writes=["own"])
            for i, v in enumerate([384.0 * 1e-6, 256.0 * 1e-6, 128.0 * 1e-5, 1e-5 / (ALPHA * ALPHA)]):
                S.op("pool", lambda e, i=i, v=v: e.memset(epsb[:, i:i + 1], float(v)), writes=["epsb"])

        def load_piece(buf, src_ap, kc, c0, n, name):
            S.dma("pool", lambda e: e.dma_start(out=buf[:, 0:kc, c0:c0 + n], in_=src_ap.rearrange("(c p) n -> p c n", p=128)),
                  writes=[name])

        wc_rr = [0]

        def load_wchunk(wbufs, src_ap, kc, ncols, tagbase):
            i = wc_rr[0] % len(wbufs)
            wc_rr[0] += 1
            buf = wbufs[i]
            name = "%s%d" % (tagbase, i)
            dst = buf[:, 0:kc, 0:ncols]
            S.dma("pool", lambda e: e.dma_start(out=dst, in_=src_ap.rearrange("(c p) n -> p c n", p=128)),
                  writes=[name])
            return buf, name

        def layer(li, xsrc, ydst):
            lam_init = 0.8 - 0.6 * math.exp(-0.3 * li)
            w_in = W["w_in"][li]

            S.barrier()
            def phase0():
                NXB = 3
                xb = [V(PH + 108 * KB + i * 2 * KB, BF16, [D]) for i in range(NXB)]
                pb = [V(PH + 114 * KB + i * 512, BF16, [256]) for i in range(NXB)]
                for tt in range(16):
                    b = xb[tt % NXB]
                    bn = "xb%d" % (tt % NXB)
                    S.dma("pool", lambda e, b=b, tt=tt: e.dma_start(out=b, in_=xsrc[tt * 128:(tt + 1) * 128, :]),
                          reads=(["Y%d" % tt] if xsrc is not x_in else []), writes=[bn])
                    bk = tt % 2

                    def tr(e, b=b, bk=bk):
                        ins = None
                        for c in range(8):
                            ins = e.transpose(out=PBb(bk)[:, c * 128:(c + 1) * 128], in_=b[:, c * 128:(c + 1) * 128],
                                              identity=ident)
                        return ins
                    S.op("pe", tr, reads=[bn, "ident"], writes=["P:%d" % bk])
                    evac(xT[:, :, tt * 128:(tt + 1) * 128],
                         PBb(bk).rearrange("p (c t) -> p c t", c=8), ["P:%d" % bk], ["xT"])
                    b2 = pb[tt % NXB]
                    b2n = "pb%d" % (tt % NXB)
                    S.dma("pool", lambda e, b2=b2, tt=tt: e.dma_start(out=b2, in_=p_in[li, tt * 128:(tt + 1) * 128, :]),
                          writes=[b2n])
                    bk2 = 2 + tt % 2

                    def tr2(e, b2=b2, bk2=bk2):
                        ins = None
                        for c in range(2):
                            ins = e.transpose(out=PBb(bk2)[:, c * 128:(c + 1) * 128], in_=b2[:, c * 128:(c + 1) * 128],
                                              identity=ident)
                        return ins
                    S.op("pe", tr2, reads=[b2n, "ident"], writes=["P:%d" % bk2])
                    evac(pT[:, :, tt * 128:(tt + 1) * 128],
                         PBb(bk2)[:, 0:256].rearrange("p (c t) -> p c t", c=2), ["P:%d" % bk2], ["pT"])


            def attention(nm, QT, KT, kparts, Vfn, scale, Qs, accs, outfn, pts, sb=((0, 1), (2, 3)), fill=None, LA=1, grp=2, fill_every=1):
                jobs = []
                for ji, Q in enumerate(Qs):
                    if isinstance(nm, list):
                        tag, QTj, KTj, qnj, knj, kpj = nm[ji]
                    else:
                        tag, qnj, knj = nm
                        QTj, KTj = QT, KT
                        kpj = kparts
                    nkt = 4 * Q + 4
                    abk = accs(ji if isinstance(nm, list) else Q)
                    for kt in range(nkt):
                        jobs.append((ji, Q, kt, nkt, abk, kpj, QTj, KTj, qnj, knj))
                n = len(jobs)
                units = []
                i = 0
                while i < n:
                    (ji, Q, kt, nkt) = jobs[i][0:4]
                    if grp == 2 and kt - 4 * Q < 0 and i + 1 < n and jobs[i + 1][0] == ji and jobs[i + 1][2] - 4 * Q < 0:
                        units.append([i, i + 1]); i += 2
                    else:
                        units.append([i]); i += 1
                nu = len(units)

                def score(u):
                    slot = sb[u % len(sb)]
                    pt = pts[u % len(pts)]
                    ptn = "pt%d" % (u % len(pts))
                    for idx, ti in enumerate(units[u]):
                        (ji, Q, kt, nkt, abk, (lo, hi), QTj, KTj, qnj, knj) = jobs[ti]
                        j = kt - 4 * Q
                        c0 = max(j, 0) * 128
                        sbk = slot[idx]
                        mms = [(PB(sbk)[:, c0:512], KTj[lo:hi, kt * 128:(kt + 1) * 128],
                                QTj[lo:hi, Q * 512 + c0:(Q + 1) * 512])]
                        rd = [qnj, knj]
                        if j >= 0:
                            mms.append((PB(sbk)[:, c0:c0 + 128], ident, tri))
                            rd += ["ident", "tri"]
                        mm_group(mms, rd, ["P:%d" % sbk])
                    if len(units[u]) == 2:
                        b0 = slot[0]
                        S.op("act", lambda e: e.activation(out=pt[:, 0:1024], in_=psall[:, b0 * 512:b0 * 512 + 1024],
                                                           func=AF.Exp, scale=scale),
                             reads=["P:%d" % slot[0], "P:%d" % slot[1]], writes=[ptn])
                    else:
                        S.op("act", lambda e: e.activation(out=pt[:, c0:512], in_=PB(sbk)[:, c0:512], func=AF.Exp, scale=scale),
                             reads=["P:%d" % sbk], writes=[ptn])

                def pvs(u):
                    pt = pts[u % len(pts)]
                    ptn = "pt%d" % (u % len(pts))
                    for idx, ti in enumerate(units[u]):
                        (ji, Q, kt, nkt, abk, kpj, QTj, KTj, qnj, knj) = jobs[ti]
                        j = kt - 4 * Q
                        c0 = max(j, 0) * 128
                        for (lhsT, lname), obk in zip(Vfn(ji if isinstance(nm, list) else Q, kt), abk):
                            S.op("pe", lambda e: e.matmul(PB(obk)[:, c0:512], lhsT, pt[:, idx * 512 + c0:(idx + 1) * 512],
                                                          start=(kt == 0), stop=(kt == nkt - 1)),
                                 reads=[ptn, lname], writes=["P:%d" % obk])
                        if kt == nkt - 1:
                            outfn(ji if isinstance(nm, list) else Q, abk)

                fcnt = [0]
                for u in range(nu + LA):
                    if u < nu:
                        score(u)
                    if fill is not None:
                        for _ in range(len(units[min(u, nu - 1)])):
                            fcnt[0] += 1
                            if fcnt[0] % fill_every == 0:
                                next(fill, None)
                    if u - LA >= 0:
                        pvs(u - LA)

            M0 = 56 * KB
            cqn = V(M0, BF16, [3, S_LEN])
            ckvn = V(M0 + 12 * KB, BF16, [2, S_LEN])
            krope = V(M0 + 20 * KB, BF16, [S_LEN])
            wukv = V(M0 + 24 * KB, BF16, [2, 1024])
            wv = V(M0 + 28 * KB, BF16, [2, 512])
            o = PH
            cs = V(o, F32, [2, S_LEN]); o += 16 * KB
            wuq = V(o, BF16, [3, 768]); o += 4608
            wuqr = V(o, BF16, [3, 768]); o += 4608
            vaug = V(o, BF16, [16, 4, 128]); o += 16 * KB
            QTs = [V(o + i * 4 * KB, BF16, [S_LEN]) for i in range(2)]; o += 8 * KB
            KTs = [V(o + i * 4 * KB, BF16, [S_LEN]) for i in range(2)]; o += 8 * KB
            pts = [V(o + i * KB, BF16, [512]) for i in range(3)]; o += 3 * KB
            wcs = [V(o + i * 8 * KB, BF16, [8, 512]) for i in range(2)]; o += 16 * KB
            sz = V(o, BF16, [S_LEN]); o += 4 * KB
            szs_b = [sz, V(o, BF16, [S_LEN])]; o += 4 * KB
            rs = [V(o + i * 2 * KB, F32, [512]) for i in range(2)]; o += 4 * KB
            rstd = V(o, F32, [512]); o += 2 * KB
            sq = V(o, BF16, [3, 512]); o += 3 * KB
            tmpa = V(o, F32, [512]); o += 2 * KB
            tmpb = V(o, F32, [512]); o += 2 * KB
            tmpc = V(o, F32, [512]); o += 2 * KB
            wkr = V(o, BF16, [8, 96]); o += 1536
            wkrr = V(o, BF16, [8, 96]); o += 1536
            gq = V(o, F32, [3]); o += 32
            gkv = V(o, F32, [2]); o += 32
            stg = V(o, F32, [1024]); o += 4 * KB
            assert o <= PH + 108 * KB, o

            wA = wcs[0]; wB = wcs[1]; wBn = "wB"
            for c in range(4):
                load_piece(wA, w_in[:, O_BCQ + c * 128:O_BCQ + (c + 1) * 128], 8, c * 128, 128, "wA%d" % c)
            load_piece(wB, w_in[:, O_BCQ + 512:O_BCQ + 672], 8, 0, 160, "wB")
            S.dma("sp", lambda e: e.dma_start(out=hb, in_=W["b_merge"][li].rearrange("(c p) -> p c", p=128),
                                              allow_slow_non_contiguous=True), writes=["hb"])
            S.op("dve", lambda e: e.tensor_scalar(out=hb, in0=hb, scalar1=0.5, scalar2=None, op0=ALU.mult), reads=["hb"], writes=["hb"])
            phase0()
            S.dma("sp", lambda e: e.dma_start(out=cs[64:96, :, :], in_=C["c_cs"].rearrange("a f s -> f a s")),
                  writes=["cs"])
            S.dma("sp", lambda e: e.dma_start(out=gq, in_=W["mla_q_norm_g"][li].rearrange("(c p) -> p c", p=128),
                                              allow_slow_non_contiguous=True), writes=["gq"])
            S.dma("sp", lambda e: e.dma_start(out=gkv, in_=W["mla_kv_norm_g"][li].rearrange("(c p) -> p c", p=128),
                                              allow_slow_non_contiguous=True), writes=["gkv"])
            S.op("dve", lambda e: e.tensor_scalar(out=gq, in0=gq, scalar1=float(math.sqrt(384.0)), scalar2=None, op0=ALU.mult),
                 reads=["gq"], writes=["gq"])
            S.op("dve", lambda e: e.tensor_scalar(out=gkv, in0=gkv, scalar1=float(math.sqrt(256.0)), scalar2=None, op0=ALU.mult),
                 reads=["gkv"], writes=["gkv"])
            for c in range(3):
                S.dma("sp", lambda e, c=c: e.dma_start(out=stg[:, 0:768], in_=W["mla_w_uq"][li, c * 128:(c + 1) * 128, :]),
                      writes=["stg"])
                S.op("dve", lambda e, c=c: e.tensor_scalar(out=wuq[:, c, :], in0=stg[:, 0:768], scalar1=gq[:, c:c + 1],
                                                         scalar2=None, op0=ALU.mult),
                     reads=["stg", "gq"], writes=["wuq"])
            for c in range(2):
                S.dma("sp", lambda e, c=c: e.dma_start(out=stg[:, 0:1024], in_=W["mla_w_ukv"][li, c * 128:(c + 1) * 128, :]),
                      writes=["stg"])
                S.op("dve", lambda e, c=c: e.tensor_scalar(out=wukv[:, c, :], in0=stg[:, 0:1024], scalar1=gkv[:, c:c + 1],
                                                         scalar2=None, op0=ALU.mult),
                     reads=["stg", "gkv"], writes=["wukv"])
            S.op("pool", lambda e: e.memset(wuqr, 0.0), writes=["wuqr"])
            for c in range(3):
                src = wuq[:, c, :].rearrange("p (h f) -> p h f", h=8)
                dst = wuqr[:, c, :].rearrange("p (h f) -> p h f", h=8)
                S.op("pool", lambda e, src=src, dst=dst: e.tensor_scalar(out=dst[:, :, 64:80], in0=src[:, :, 80:96],
                                                                        scalar1=-1.0, scalar2=None, op0=ALU.mult),
                     reads=["wuq"], writes=["wuqr"])
                S.op("pool", lambda e, src=src, dst=dst: e.tensor_copy(out=dst[:, :, 80:96], in_=src[:, :, 64:80]),
                     reads=["wuq"], writes=["wuqr"])
            for c in range(2):
                src = wukv[:, c, :].rearrange("p (h f) -> p h f", h=8)
                dst = wv[:, c, :].rearrange("p (h f) -> p h f", h=8)
                S.op("pool", lambda e, src=src, dst=dst: e.tensor_copy(out=dst, in_=src[:, :, 64:128]),
                     reads=["wukv"], writes=["wv"])

            S.op("pool", lambda e: e.memset(wkr, 0.0), writes=["wkr"])
            S.op("pool", lambda e: e.memset(wkrr, 0.0), writes=["wkrr"])
            S.op("pool", lambda e: e.tensor_copy(out=wkr[:, :, 64:96], in_=wB[:, :, 128:160]), reads=[wBn], writes=["wkr"])
            S.op("pool", lambda e: e.tensor_scalar(out=wkrr[:, :, 64:80], in0=wB[:, :, 144:160], scalar1=-1.0,
                                                  scalar2=None, op0=ALU.mult), reads=[wBn], writes=["wkrr"])
            S.op("pool", lambda e: e.tensor_copy(out=wkrr[:, :, 80:96], in_=wB[:, :, 128:144]), reads=[wBn], writes=["wkrr"])

            def rms_group(T, nch, wsel, dst, eps_n):
                for c in range(nch):
                    wbuf, wn, col = wsel(c)
                    mm_group([(PB(c), wbuf[:, k, col:col + 128], xT[:, k, T * 512:(T + 1) * 512]) for k in range(8)],
                             [wn, "xT"], ["P:%d" % c])
                    S.op("act", lambda e, c=c: e.activation(out=sq[:, c, :], in_=PB(c), func=AF.Square),
                         reads=["P:%d" % c], writes=["sq%d" % c])
                mm_group([(PB(3), ones, sq[:, c, :]) for c in range(nch)], ["ones"] + ["sq%d" % c for c in range(nch)], ["P:3"])
                S.op("act", lambda e: e.activation(out=rstd, in_=PB(3), func=AF.Sqrt, bias=epsb[:, 0:1] if eps_n > 3e-4 else epsb[:, 1:2]),
                     reads=["P:3", "epsb"], writes=["rstd"])
                S.op("dve", lambda e: e.reciprocal(out=rstd, in_=rstd), reads=["rstd"], writes=["rstd"])
                for c in range(nch):
                    S.op("dve", lambda e, c=c: e.tensor_tensor(out=dst[:, c, T * 512:(T + 1) * 512], in0=PB(c), in1=rstd,
                                                              op=ALU.mult), reads=["P:%d" % c, "rstd"], writes=[dst.name if False else "nrm"])

            for T in range(4):
                rms_group(T, 3, lambda c: (wA, "wA%d" % c, c * 128), cqn, 384.0 * 1e-6)
                rms_group(T, 2, lambda c: ((wA, "wA3", 384) if c == 0 else (wB, wBn, 0)), ckvn, 256.0 * 1e-6)
                mm_group([(PB(4)[0:96, :], wkr[:, k, :], xT[:, k, T * 512:(T + 1) * 512]) for k in range(8)],
                         ["wkr", "xT"], ["P:4"])
                mm_group([(PB(5)[0:96, :], wkrr[:, k, :], xT[:, k, T * 512:(T + 1) * 512]) for k in range(8)],
                         ["wkrr", "xT"], ["P:5"])
                S.op("dve", lambda e, T=T: e.tensor_tensor(out=tmpa[64:96, :], in0=PB(4)[64:96, :],
                                                          in1=cs[64:96, 0, T * 512:(T + 1) * 512], op=ALU.mult),
                     reads=["P:4", "cs"], writes=["tmpa"])
                S.op("dve", lambda e, T=T: e.tensor_tensor(out=tmpb[64:96, :], in0=PB(5)[64:96, :],
                                                          in1=cs[64:96, 1, T * 512:(T + 1) * 512], op=ALU.mult),
                     reads=["P:5", "cs"], writes=["tmpb"])
                S.op("dve", lambda e, T=T: e.tensor_tensor(out=krope[64:96, T * 512:(T + 1) * 512], in0=tmpa[64:96, :],
                                                          in1=tmpb[64:96, :], op=ALU.add),
                     reads=["tmpa", "tmpb"], writes=["krope"])

            if "d_cqn" in dbg:
                for c in range(3):
                    S.op("dve", lambda e, c=c: e.tensor_copy(out=stg[:, 0:512], in_=cqn[:, c, 0:512]), reads=["nrm"], writes=["stg"])
                    S.dma("sp", lambda e, c=c: e.dma_start(out=dbg["d_cqn"][c * 128:(c + 1) * 128, :], in_=stg[:, 0:512]), reads=["stg"])

            wZ = wcs[0]
            for c in range(4):
                load_piece(wZ, w_in[:, O_BZ + c * 128:O_BZ + (c + 1) * 128], 8, c * 128, 128, "wA%d" % c)

            S.op("pool", lambda e: e.memset(vaug, 2.0), writes=["vaug"])
            SC_B = 96.0 ** -0.5
            def mla_v(hg):
                for tt in range(16):
                    bk = 6 + tt % 2
                    mm_group([(PB(bk)[:, 0:256], ckvn[:, k, tt * 128:(tt + 1) * 128], wv[:, k, hg * 256:(hg + 1) * 256])
                              for k in range(2)], ["nrm", "wv"], ["P:%d" % bk])
                    for par in range(2):
                        src = PB(bk)[:, 0:256].rearrange("p (h f) -> p h f", h=4)[:, par::2, :]
                        dst = vaug[:, tt, par::2, par * 64:par * 64 + 64]
                        evac(dst, src, ["P:%d" % bk], ["vaug"])

            def prep_b(h):
                QT = QTs[h % 2]; KT = KTs[h % 2]
                qn = "QT%d" % (h % 2); kn = "KT%d" % (h % 2)
                par = h % 2
                szp = szs_b[(h // 2) % 2]; szn = "szb%d" % ((h // 2) % 2)
                for T in range(4):
                    mm_group([(PB(6)[0:96, :], wuq[:, k, h * 96:(h + 1) * 96], cqn[:, k, T * 512:(T + 1) * 512])
                              for k in range(3)], ["wuq", "nrm"], ["P:6"])
                    mm_group([(PB(7)[0:96, :], wuqr[:, k, h * 96:(h + 1) * 96], cqn[:, k, T * 512:(T + 1) * 512])
                              for k in range(3)], ["wuqr", "nrm"], ["P:7"])
                    evac(QT[0:64, T * 512:(T + 1) * 512], PB(6)[0:64, :], ["P:6"], [qn], eng="dve")
                    S.op("dve", lambda e: e.tensor_tensor(out=tmpa[64:96, :], in0=PB(6)[64:96, :],
                                                         in1=cs[64:96, 0, T * 512:(T + 1) * 512], op=ALU.mult),
                         reads=["P:6", "cs"], writes=["tmpa"])
                    S.op("dve", lambda e: e.tensor_tensor(out=tmpb[64:96, :], in0=PB(7)[64:96, :],
                                                         in1=cs[64:96, 1, T * 512:(T + 1) * 512], op=ALU.mult),
                         reads=["P:7", "cs"], writes=["tmpb"])
                    S.op("pool", lambda e: e.tensor_tensor(out=QT[64:96, T * 512:(T + 1) * 512],
                                                          in0=tmpa[64:96, :], in1=tmpb[64:96, :], op=ALU.add),
                         reads=["tmpa", "tmpb"], writes=[qn])
                    yield
                    mm_group([(PB(3)[0:64, :], wukv[:, k, h * 128:h * 128 + 64], ckvn[:, k, T * 512:(T + 1) * 512])
                              for k in range(2)], ["wukv", "nrm"], ["P:3"])
                    evac(KT[0:64, T * 512:(T + 1) * 512], PB(3)[0:64, :], ["P:3"], [kn], eng="act")
                    yield
                    if par == 0:
                        mm_group([(PB(3), wZ[:, k, (h // 2) * 128:(h // 2 + 1) * 128], xT[:, k, T * 512:(T + 1) * 512]) for k in range(8)],
                                 ["wA%d" % (h // 2), "xT"], ["P:3"])
                        S.op("act", lambda e: e.activation(out=tmpc, in_=PB(3), func=AF.Tanh, scale=0.5),
                             reads=["P:3"], writes=["tmpc"])
                        S.op("dve", lambda e: e.scalar_tensor_tensor(
                            out=szp[:, T * 512:(T + 1) * 512], in0=tmpc, scalar=1.0, in1=PB(3),
                            op0=ALU.add, op1=ALU.mult), reads=["tmpc", "P:3"], writes=[szn])
                        yield
                S.op("pool", lambda e: e.tensor_copy(out=KT[64:96, :], in_=krope[64:96, :]),
                     reads=["krope"], writes=[kn])
                yield

            def out_b(h):
                par = h % 2
                sz = szs_b[(h // 2) % 2]
                szn = "szb%d" % ((h // 2) % 2)

                def outfn(Q, abk):
                    obk = abk[0]
                    vr = slice(par * 64, par * 64 + 64)
                    sr = slice((1 - par) * 64, (1 - par) * 64 + 64)
                    r = rs[Q % 2]; rn = "rs%d" % (Q % 2)
                    S.op("dve", lambda e: e.reciprocal(out=r[vr, :], in_=PB(obk)[sr, :]), reads=["P:%d" % obk], writes=[rn])
                    S.op("pool", lambda e: e.tensor_tensor(out=r[vr, :], in0=r[vr, :], in1=sz[vr, Q * 512:(Q + 1) * 512], op=ALU.mult),
                         reads=[rn, szn], writes=[rn])
                    S.op("dve", lambda e: e.tensor_tensor(out=ybT[vr, h // 2, Q * 512:(Q + 1) * 512], in0=PB(obk)[vr, :],
                                                         in1=r[vr, :], op=ALU.mult), reads=["P:%d" % obk, rn], writes=["ybT"])
                return outfn

            for _ in prep_b(0):
                pass
            for h in range(8):
                if h % 4 == 0:
                    mla_v(h // 4)
                nxt = prep_b(h + 1) if h < 7 else None

                def Vfn(Q, kt, hh=h % 4):
                    return [(vaug[:, kt, hh, :], "vaug")]
                attention(("b", "QT%d" % (h % 2), "KT%d" % (h % 2)), QTs[h % 2], KTs[h % 2], (0, 96), Vfn, SC_B, range(4),
                          lambda Q: [4 + Q % 2], out_b(h), pts, sb=((0,), (1,), (2,)), fill=nxt, LA=2, grp=1, fill_every=2)
                if nxt is not None:
                    for _ in nxt:
                        pass

            def dump(name, srcT, nchunks, rname):
                if name in dbg:
                    for c in range(nchunks):
                        for T in range(4):
                            S.op("dve", lambda e, c=c, T=T: e.tensor_copy(out=dstg[:, 0:512], in_=srcT[:, c, T * 512:(T + 1) * 512]),
                                 reads=[rname], writes=["dstg"])
                            S.dma("sp", lambda e, c=c, T=T: e.dma_start(out=dbg[name][c * 128:(c + 1) * 128, T * 512:(T + 1) * 512],
                                                                       in_=dstg[:, 0:512]), reads=["dstg"])
            dstg = stg
            dump("d_yb", ybT, 4, "ybT")

            if UPTO <= 1:
                return
            S.barrier()
            o = PH
            vaug = V(o, BF16, [16, 4, 128]); o += 16 * KB
            QTs = [[V(o + (2 * i + j) * 4 * KB, BF16, [S_LEN]) for j in range(2)] for i in range(2)]; o += 16 * KB
            KTs = [[V(o + (2 * i + j) * 4 * KB, BF16, [S_LEN]) for j in range(2)] for i in range(2)]; o += 16 * KB
            pts = [V(o + i * 2 * KB, BF16, [1024]) for i in range(3)]
            pts6 = [V(o + i * KB, BF16, [512]) for i in range(6)]; o += 6 * KB
            wq4 = [V(o + i * 8 * KB, BF16, [8, 512]) for i in range(4)]; o += 32 * KB
            szs = [V(o + i * 4 * KB, BF16, [S_LEN]) for i in range(2)]; o += 8 * KB
            rs = [V(o + i * 2 * KB, F32, [512]) for i in range(2)]; o += 4 * KB
            tmpa = V(o, F32, [512]); o += 2 * KB
            tmpb = V(o, F32, [512]); o += 2 * KB
            tmpc = V(o, F32, [512]); o += 2 * KB
            cmpb = V(o - 4 * KB, F32, [1024])
            sqb = V(o, BF16, [512]); o += KB
            tmpas = [tmpa, V(o, F32, [512])]; o += 2 * KB
            tmpcs = [tmpc, V(o, F32, [512])]; o += 2 * KB
            sqbs = [sqb, V(o, BF16, [512])]; o += KB
            gt = V(o, F32, [128]); o += 512
            m8 = V(o, F32, [128]); o += 512
            msk = V(o, F32, [128]); o += 512
            mbb = V(o, BF16, [128]); o += 256
            ksum = V(o, F32, [16]); o += 64
            ksb = V(o, BF16, [16]); o += 32
            dstg = V(o, F32, [512]); o += 2 * KB
            lt = V(o, F32, [256]); o += KB
            lsm = V(o, F32, [8]); o += 32
            gs = V(o, F32, [2]); o += 32
            assert o <= 206 * KB, o
            KP = [(0, 80), (0, 80)]
            DR = [slice(0, 64), slice(0, 64)]
            PR = [slice(0, 64), slice(64, 128)]
            MR = [slice(64, 72), slice(64, 72)]
            AR = [slice(72, 80), slice(72, 80)]
            ZR = [slice(64, 96), slice(64, 96)]

            def tn(kind, i, j):
                return "%s%d%d" % (kind, i, j)

            wcs4 = wq4
            wc_rr[0] = 0
            wQ, wK, wV, wZ = wq4

            def load_pair(oq, ok, oz, pr):
                load_piece(wQ, w_in[:, oq + pr * 128:oq + (pr + 1) * 128], 8, pr * 128, 128, "w4q%d" % pr)
                load_piece(wK, w_in[:, ok + pr * 128:ok + (pr + 1) * 128], 8, pr * 128, 128, "w4k%d" % pr)
                load_piece(wZ, w_in[:, oz + pr * 128:oz + (pr + 1) * 128], 8, pr * 128, 128, "w4z%d" % pr)

            def load_v(ov, g):
                load_piece(wV, w_in[:, ov + g * 256:ov + (g + 1) * 256], 8, g * 256, 256, "w4v%d" % g)
            load_pair(O_AQ, O_AK, O_AZ, 0)
            for i in range(2):
                for j in range(2):
                    S.op("dve", lambda e: e.memset(QTs[i][j][ZR[j], :], 0.0), writes=[tn("QT", i, j)])
                    S.op("dve", lambda e: e.memset(KTs[i][j][ZR[j], :], 0.0), writes=[tn("KT", i, j)])
                    S.dma("pool", lambda e: e.dma_start(out=KTs[i][j][MR[j], :], in_=C["c_ind"]), writes=[tn("KT", i, j)])

            def alibi_a(pr):
                i = pr % 2
                for j in range(2):
                    h = 2 * pr + j
                    S.dma("pool", lambda e: e.dma_start(out=QTs[i][j][AR[j], :], in_=C["c_aq"][h]), writes=[tn("QT", i, j)])
                    S.dma("pool", lambda e: e.dma_start(out=KTs[i][j][AR[j], :], in_=C["c_ak"][h]), writes=[tn("KT", i, j)])
            alibi_a(0)
            load_v(O_AV, 0)
            load_pair(O_AQ, O_AK, O_AZ, 1)
            S.op("pool", lambda e: e.memset(vaug, 2.0), writes=["vaug"])

            def zgate(wbuf, wn, col, sz_, szn, T):
                mm_group([(PB(7), wbuf[:, k, col:col + 128], xT[:, k, T * 512:(T + 1) * 512]) for k in range(8)],
                         [wn, "xT"], ["P:7"])
                S.op("act", lambda e: e.activation(out=tmpa, in_=PB(7), func=AF.Tanh, scale=0.5),
                     reads=["P:7"], writes=["tmpa"])
                S.op("dve", lambda e: e.scalar_tensor_tensor(out=sz_[:, T * 512:(T + 1) * 512], in0=tmpa, scalar=1.0,
                                                            in1=PB(7), op0=ALU.add, op1=ALU.mult),
                     reads=["tmpa", "P:7"], writes=[szn])

            def v_tokmajor(src_sel, wbuf, wn, col0, ncols, dst_fn):
                for tt in range(16):
                    bk = 6 + tt % 2
                    mm_group([(PB(bk)[:, 0:ncols], xT[:, k, tt * 128:(tt + 1) * 128], wbuf[:, k, col0:col0 + ncols])
                              for k in range(8)], ["xT"] + list(wn), ["P:%d" % bk])
                    dst_fn(tt, bk)

            def norm_out(yT_, ynm, h, par, Q, obk, sz_, szn):
                vr = slice(par * 64, par * 64 + 64)
                sr = slice((1 - par) * 64, (1 - par) * 64 + 64)
                r = rs[Q % 2]; rn = "rs%d" % (Q % 2)
                S.op("dve", lambda e: e.reciprocal(out=r[vr, :], in_=PB(obk)[sr, :]), reads=["P:%d" % obk], writes=[rn])
                S.op("pool", lambda e: e.tensor_tensor(out=r[vr, :], in0=r[vr, :], in1=sz_[vr, Q * 512:(Q + 1) * 512], op=ALU.mult),
                     reads=[rn, szn], writes=[rn])
                S.op("dve", lambda e: e.tensor_tensor(out=yT_[vr, h // 2, Q * 512:(Q + 1) * 512], in0=PB(obk)[vr, :],
                                                     in1=r[vr, :], op=ALU.mult), reads=["P:%d" % obk, rn], writes=[ynm])

            def vdst(tt, bk):
                for par in range(2):
                    src = PB(bk)[:, 0:256].rearrange("p (h f) -> p h f", h=4)[:, par::2, :]
                    dst = vaug[:, tt, par::2, par * 64:par * 64 + 64]
                    evac(dst, src, ["P:%d" % bk], ["vaug"])

            def qkz_items(i, wqn, wkn, wzn, col, sz_, szn):
                items = []
                for T in range(4):
                    def pe_q(bk, T=T):
                        mm_group([(PB(bk), wQ[:, k, col:col + 128], xT[:, k, T * 512:(T + 1) * 512]) for k in range(8)],
                                 [wqn, "xT"], ["P:%d" % bk])

                    def post_q(bk, T=T):
                        for j in range(2):
                            evac(QTs[i][j][DR[j], T * 512:(T + 1) * 512], PB(bk)[PR[j], :], ["P:%d" % bk], [tn("QT", i, j)], eng="dve")

                    def pe_k(bk, T=T):
                        mm_group([(PB(bk), wK[:, k, col:col + 128], xT[:, k, T * 512:(T + 1) * 512]) for k in range(8)],
                                 [wkn, "xT"], ["P:%d" % bk])

                    def post_k(bk, T=T):
                        for j in range(2):
                            evac(KTs[i][j][DR[j], T * 512:(T + 1) * 512], PB(bk)[PR[j], :], ["P:%d" % bk], [tn("KT", i, j)], eng="act")

                    def pe_z(bk, T=T):
                        mm_group([(PB(bk), wZ[:, k, col:col + 128], xT[:, k, T * 512:(T + 1) * 512]) for k in range(8)],
                                 [wzn, "xT"], ["P:%d" % bk])

                    def post_z(bk, T=T):
                        S.op("act", lambda e: e.activation(out=tmpa, in_=PB(bk), func=AF.Tanh, scale=0.5),
                             reads=["P:%d" % bk], writes=["tmpa"])
                        S.op("dve", lambda e: e.scalar_tensor_tensor(out=sz_[:, T * 512:(T + 1) * 512], in0=tmpa, scalar=1.0,
                                                                    in1=PB(bk), op0=ALU.add, op1=ALU.mult),
                             reads=["tmpa", "P:%d" % bk], writes=[szn])
                    items += [(pe_q, post_q), (pe_k, post_k), (pe_z, post_z)]
                return items

            def skewed(items, fb=(6, 7)):
                prev = None
                for k, (pe, post) in enumerate(items):
                    bk = fb[k % len(fb)]
                    pe(bk)
                    if prev is not None:
                        prev[0](prev[1])
                    prev = (post, bk)
                    yield
                prev[0](prev[1])
                yield

            def prep_a(pr):
                i = pr % 2
                sz_ = szs[i]; szn = "sz%d" % i
                if pr >= 1:
                    alibi_a(pr)
                    if pr + 1 < 4:
                        load_pair(O_AQ, O_AK, O_AZ, pr + 1)
                    if pr == 1:
                        load_v(O_AV, 1)
                for _ in skewed(qkz_items(i, "w4q%d" % pr, "w4k%d" % pr, "w4z%d" % pr, pr * 128, sz_, szn)):
                    yield
                for j in range(2):
                    S.op("dve", lambda e: e.tensor_reduce(out=ksum[0:64, j * 8:(j + 1) * 8],
                                                         in_=KTs[i][j][0:64, :].rearrange("p (n k) -> p n k", n=8),
                                                         axis=AX.X, op=ALU.add), reads=[tn("KT", i, j)], writes=["ksum"])
                S.op("dve", lambda e: e.tensor_copy(out=ksb[0:64, :], in_=ksum[0:64, :]), reads=["ksum"], writes=["ksb"])
                for _ in range(10):
                    yield

                def gmm(e):
                    for j in range(2):
                        for qi in range(8):
                            e.matmul(PB(6)[:, j * 64 + qi * 8:j * 64 + (qi + 1) * 8], QTs[i][j][0:64, (8 + qi) * 128:(9 + qi) * 128],
                                     ksb[0:64, j * 8:(j + 1) * 8], start=True, stop=True)
                S.op("pe", gmm, reads=[tn("QT", i, 0), tn("QT", i, 1), "ksb"], writes=["P:6"])
                yield
                yield
                S.op("dve", lambda e: e.tensor_tensor(out=gt, in0=PB(6)[:, 0:128], in1=pen, op=ALU.add),
                     reads=["P:6", "pen"], writes=["gt"])
                yield
                g0 = gt[:, 0:1]
                pst = g0.ap[0][0]
                in_m = bass.AP(g0.tensor, g0.offset, [[pst, 128], [8, 16], [0, 8], [1, 8]])
                in_n = bass.AP(g0.tensor, g0.offset, [[pst, 128], [8, 16], [1, 8], [0, 8]])
                S.op("dve", lambda e: e.tensor_tensor(out=cmpb.rearrange("p (g n m) -> p g n m", g=16, n=8), in0=in_m, in1=in_n,
                                                     op=ALU.is_gt), reads=["gt"], writes=["cmpb"])
                yield
                S.op("dve", lambda e: e.tensor_reduce(out=m8, in_=cmpb.rearrange("p (a m) -> p a m", m=8), axis=AX.X, op=ALU.add),
                     reads=["cmpb"], writes=["m8"])
                yield
                S.op("dve", lambda e: e.tensor_scalar(out=msk, in0=m8, scalar1=2.5, scalar2=None, op0=ALU.is_lt),
                     reads=["m8"], writes=["msk"])
                yield
                S.op("dve", lambda e: e.tensor_tensor(out=msk, in0=msk, in1=own, op=ALU.max), reads=["msk", "own"], writes=["msk"])
                S.op("dve", lambda e: e.tensor_scalar(out=mbb, in0=msk, scalar1=1.0, scalar2=-NEGBIG, op0=ALU.subtract, op1=ALU.mult),
                     reads=["msk"], writes=["mbb"])
                for _ in range(6):
                    yield
                for j in range(2):
                    def gtr(e):
                        for qi in range(8):
                            e.transpose(out=PBb(7)[0:8, qi * 128:(qi + 1) * 128], in_=mbb[:, j * 64 + qi * 8:j * 64 + (qi + 1) * 8],
                                        identity=ident)
                    S.op("pe", gtr, reads=["mbb", "ident"], writes=["P:7"])
                    evac(QTs[i][j][MR[j], 1024:2048], PBb(7)[0:8, 0:1024], ["P:7"], [tn("QT", i, j)], eng="dve")
                    yield

            for _ in prep_a(0):
                pass
            for pr in range(4):
                i = pr % 2
                nxt = prep_a(pr + 1) if pr < 3 else None
                for j in range(2):
                    h = 2 * pr + j
                    if h % 4 == 0:
                        v_tokmajor(None, wV, ["w4v%d" % (h // 4)], (h // 4) * 256, 256, vdst)

                    def Vfn(Q, kt, hh=h % 4):
                        return [(vaug[:, kt, hh, :], "vaug")]
                    attention(("a", tn("QT", i, j), tn("KT", i, j)), QTs[i][j], KTs[i][j], KP[j], Vfn, 0.125, range(4),
                              lambda Q: [4 + Q % 2],
                              lambda Q, abk, h=h, j=j, i=i: norm_out(yaT, "yaT", h, j, Q, abk[0], szs[i], "sz%d" % i), pts6, sb=((0,), (1,), (2,), (3,)), fill=nxt, LA=3, grp=1, fill_every=1)
                if nxt is not None:
                    for _ in nxt:
                        pass
            dump("d_ya", yaT, 4, "yaT")

            if UPTO <= 2:
                return
            S.barrier()
            wc_rr[0] = 0
            def alibi_c(h):
                i = h % 2
                for j in range(2):
                    S.dma("pool", lambda e: e.dma_start(out=QTs[i][j][AR[j], :], in_=C["c_aq"][8 + h]), writes=[tn("QT", i, j)])
                    S.dma("pool", lambda e: e.dma_start(out=KTs[i][j][AR[j], :], in_=C["c_ak"][8 + h]), writes=[tn("KT", i, j)])

            for g in range(2):
                load_v(O_CV, g)
            load_pair(O_CQ, O_CK, O_CZ, 0)
            for i in range(2):
                for j in range(2):
                    S.op("dve", lambda e: e.memset(QTs[i][j][ZR[j], :], 0.0), writes=[tn("QT", i, j)])
            alibi_c(0)
            load_pair(O_CQ, O_CK, O_CZ, 1)
            S.dma("sp", lambda e: e.dma_start(out=lt, in_=W["diff_lambda"][li].rearrange("a b -> (a b)").partition_broadcast(128)),
                  writes=["lt"])
            S.op("dve", lambda e: e.tensor_tensor(out=lt[:, 0:64], in0=lt[:, 0:64], in1=lt[:, 64:128], op=ALU.mult), reads=["lt"], writes=["lt"])
            S.op("dve", lambda e: e.tensor_tensor(out=lt[:, 128:192], in0=lt[:, 128:192], in1=lt[:, 192:256], op=ALU.mult), reads=["lt"], writes=["lt"])
            S.op("dve", lambda e: e.tensor_reduce(out=lsm[:, 0:2], in_=lt.rearrange("p (a b) -> p a b", a=2)[:, :, 0:64], axis=AX.X, op=ALU.add),
                 reads=["lt"], writes=["lsm"])
            S.op("act", lambda e: e.activation(out=lsm[:, 2:4], in_=lsm[:, 0:2], func=AF.Exp), reads=["lsm"], writes=["lsm"])
            S.op("dve", lambda e: e.scalar_tensor_tensor(out=lsm[:, 4:5], in0=lsm[:, 3:4], scalar=float(-lam_init), in1=lsm[:, 2:3],
                                                        op0=ALU.add, op1=ALU.subtract), reads=["lsm"], writes=["lsm"])
            S.dma("sp", lambda e: e.dma_start(out=gs[:, 0:1], in_=W["diff_subln_g"][li].rearrange("(p a) -> p a", a=1)), writes=["gs"])
            S.op("dve", lambda e: e.tensor_scalar(out=gs[:, 0:1], in0=gs[:, 0:1], scalar1=float(0.5 * math.sqrt(128.0) * (1.0 - lam_init)),
                                                 scalar2=None, op0=ALU.mult), reads=["gs"], writes=["gs"])
            vd = vaug.rearrange("p a b c -> p a (b c)")

            def vdst2(tt, bk):
                evac(vd[:, tt, :], PB(bk), ["P:%d" % bk], ["vaug"])
            v_tokmajor(None, wV, ["w4v0", "w4v1"], 0, 512, vdst2)

            def prep_c(h):
                i = h % 2
                if h + 1 < 4:
                    alibi_c(h + 1)
                if 1 <= h and h + 1 < 4:
                    load_pair(O_CQ, O_CK, O_CZ, h + 1)
                for _ in skewed(qkz_items(i, "w4q%d" % h, "w4k%d" % h, "w4z%d" % h, h * 128, szs[i], "sz%d" % i), fb=(0, 1, 2)):
                    yield

            dq = collections.deque()

            def filler(g):
                while True:
                    if dq:
                        f = dq.popleft()
                        if f is not None:
                            f()
                    if g is not None:
                        next(g, None)
                    yield

            for h in range(4):
                i = h % 2
                sz_ = szs[i]; szn = "sz%d" % i
                for _ in prep_c(h):
                    pass
                nxt = None

                def Vfn(ji, kt, h=h):
                    return [(vd[:, kt, h * 128:(h + 1) * 128], "vaug"), (ones, "ones")]

                def comb(ji, abk, h=h, sz_=sz_, szn=szn):
                    if ji % 2 == 0:
                        return
                    Q = ji // 2
                    ta = tmpas[Q % 2]; tan = "tmpa%d" % (Q % 2)
                    tc = tmpcs[Q % 2]; tcn = "tmpc%d" % (Q % 2)
                    sqq = sqbs[Q % 2]; sqn = "sqb%d" % (Q % 2)
                    S.op("dve", lambda e: e.reciprocal(out=ta, in_=PB(6)), reads=["P:6"], writes=[tan])
                    S.op("dve", lambda e: e.tensor_tensor(out=ta, in0=PB(4), in1=ta, op=ALU.mult), reads=["P:4", tan], writes=[tan])
                    S.op("dve", lambda e: e.reciprocal(out=tmpb, in_=PB(7)), reads=["P:7"], writes=["tmpb"])
                    S.op("dve", lambda e: e.tensor_tensor(out=tmpb, in0=PB(5), in1=tmpb, op=ALU.mult), reads=["P:5", "tmpb"], writes=["tmpb"])
                    S.op("dve", lambda e: e.scalar_tensor_tensor(out=ta, in0=tmpb, scalar=lsm[:, 4:5], in1=ta, op0=ALU.mult, op1=ALU.add),
                         reads=[tan, "tmpb", "lsm"], writes=[tan])
                    S.op("dve", lambda e: e.tensor_tensor(out=sqq, in0=ta, in1=ta, op=ALU.mult), reads=[tan], writes=[sqn])

                    def stage2():
                        mm_group([(PB(3), ones, sqq)], ["ones", sqn], ["P:3"])
                        S.op("act", lambda e: e.activation(out=tc, in_=PB(3), func=AF.Sqrt, bias=epsb[:, 2:3]), reads=["P:3", "epsb"], writes=[tcn])
                        S.op("dve", lambda e: e.reciprocal(out=tc, in_=tc), reads=[tcn], writes=[tcn])
                        S.op("dve", lambda e: e.tensor_tensor(out=tc, in0=tc, in1=sz_[:, Q * 512:(Q + 1) * 512], op=ALU.mult),
                             reads=[tcn, szn], writes=[tcn])
                        S.op("dve", lambda e: e.scalar_tensor_tensor(out=ycT[:, h, Q * 512:(Q + 1) * 512], in0=ta, scalar=gs[:, 0:1],
                                                                    in1=tc, op0=ALU.mult, op1=ALU.mult),
                             reads=[tan, tcn, "gs"], writes=["ycT"])
                    dq.extend([None] * 8 + [stage2])
                nm = [("c", QTs[i][ji % 2], KTs[i][ji % 2], tn("QT", i, ji % 2), tn("KT", i, ji % 2), KP[ji % 2]) for ji in range(8)]
                attention(nm, None, None, None, Vfn, 0.125, [ji // 2 for ji in range(8)],
                          lambda ji: [4 + ji % 2, 6 + ji % 2], comb, pts, sb=((0,), (1,), (2,)), fill=filler(nxt), LA=2, grp=1)
                while dq:
                    f = dq.popleft()
                    if f is not None:
                        f()
                if nxt is not None:
                    for _ in nxt:
                        pass
            dump("d_yc", ycT, 4, "ycT")

            if UPTO <= 3:
                return
            S.barrier()
            o = PH
            mergedT = V(o, BF16, [8, S_LEN]); o += 32 * KB
            o_tail = o
            wbr = [V(o + i * 8 * KB, BF16, [4, 1024]) for i in range(3)]; o += 24 * KB
            wmj = [V(o + i * 6 * KB, BF16, [8, 384]) for i in range(2)]; o += 12 * KB
            tts = [V(o + i * 2 * KB, F32, [512]) for i in range(3)]; o += 6 * KB
            mts = [V(o + i * 2 * KB, F32, [512]) for i in range(3)]; o += 6 * KB
            dstg = V(o, F32, [512]); o += 2 * KB
            assert o <= PH + 84 * KB, o
            w_o = V(PH + 84 * KB, BF16, [8, 1024])
            w_pg = V(PH + 100 * KB, BF16, [8, 1024])
            yTs = [(yaT, "yaT"), (ybT, "ybT"), (ycT, "ycT")]
            WBR = ["w_branch_a", "w_branch_b", "w_branch_c"]

            def merge_loads(j):
                wm = wmj[j % 2]; wmn = "wmj%d" % (j % 2)
                for br in range(3):
                    load_piece(wbr[br], W[WBR[br]][li][:, j * 128:(j + 1) * 128], 4, j * 128, 128, "wbr%d_%d" % (br, j))
                    S.dma("pool", lambda e: e.dma_start(
                        out=wm[:, :, br * 128:(br + 1) * 128],
                        in_=W["w_merge"][li][:, br * 1024 + j * 128:br * 1024 + (j + 1) * 128].rearrange("(c p) n -> p c n", p=128)),
                        writes=[wmn])

            merge_loads(0)
            for j in range(8):
                if j + 1 < 8:
                    merge_loads(j + 1)
                if j == 6:
                    S.dma("pool", lambda e: e.dma_start(out=w_o, in_=W["w_out"][li].rearrange("(c p) n -> p c n", p=128)), writes=["w_o"])
                    S.dma("pool", lambda e: e.dma_start(out=w_pg, in_=W["w_ple_gate"][li].rearrange("(c p) n -> p c n", p=128)), writes=["w_pg"])
                wm = wmj[j % 2]; wmn = "wmj%d" % (j % 2)
                for T in range(4):
                    for br in range(3):
                        yT_, ynm = yTs[br]
                        mm_group([(PB(br), wbr[br][:, k, j * 128:(j + 1) * 128], yT_[:, k, T * 512:(T + 1) * 512]) for k in range(4)],
                                 ["wbr%d_%d" % (br, j), ynm], ["P:%d" % br])
                        mm_group([(PB(3 + br), wm[:, k, br * 128:(br + 1) * 128], xT[:, k, T * 512:(T + 1) * 512]) for k in range(8)],
                                 [wmn, "xT"], ["P:%d" % (3 + br)])
                        S.op("act", lambda e: e.activation(out=tts[br], in_=PB(3 + br), func=AF.Tanh, scale=0.5,
                                                           bias=hb[:, br * 8 + j:br * 8 + j + 1]),
                             reads=["P:%d" % (3 + br), "hb"], writes=["tt%d" % br])
                        S.op("dve", lambda e: e.scalar_tensor_tensor(out=mts[br], in0=tts[br], scalar=1.0, in1=PB(br),
                                                                    op0=ALU.add, op1=ALU.mult),
                             reads=["tt%d" % br, "P:%d" % br], writes=["mt%d" % br])
                    S.op("dve", lambda e: e.tensor_tensor(out=mts[0], in0=mts[0], in1=mts[1], op=ALU.add), reads=["mt0", "mt1"], writes=["mt0"])
                    S.op("dve", lambda e: e.tensor_tensor(out=mergedT[:, j, T * 512:(T + 1) * 512], in0=mts[0], in1=mts[2], op=ALU.add),
                         reads=["mt0", "mt2"], writes=["mergedT"])
            dump("d_mg", mergedT, 8, "mergedT")

            if UPTO <= 4:
                return
            S.barrier()
            o = o_tail
            w_p = V(o, BF16, [2, 1024]); o += 4 * KB
            lng = V(o, F32, [1024]); o += 4 * KB
            lnb = V(o, F32, [1024]); o += 4 * KB
            xts = [V(o + i * 4 * KB, F32, [1024]) for i in range(2)]; o += 8 * KB
            rps = [V(40 * KB + i * 4 * KB, F32, [1024]) for i in range(5)]
            rbs = [V(o + i * 2 * KB, BF16, [1024]) for i in range(2)]; o += 4 * KB
            rTs = [V(o + i * 2 * KB, BF16, [8, 128]) for i in range(2)]; o += 4 * KB
            ths = [V(o + i * 2 * KB, F32, [512]) for i in range(2)]; o += 4 * KB
            pps = [V(o + i * 2 * KB, F32, [512]) for i in range(2)]; o += 4 * KB
            ys = [V(o + i * 4 * KB, F32, [1024]) for i in range(3)]; o += 12 * KB
            st8s = [V(o + i * 32, F32, [8]) for i in range(3)]; o += 96
            assert o <= PH + 84 * KB, o
            S.dma("pool", lambda e: e.dma_start(out=w_p, in_=W["w_ple"][li].rearrange("(c p) n -> p c n", p=128)), writes=["w_p"])
            S.dma("sp", lambda e: e.dma_start(out=lng, in_=W["ln_g"][li].partition_broadcast(128)), writes=["lng"])
            S.dma("sp", lambda e: e.dma_start(out=lnb, in_=W["ln_b"][li].partition_broadcast(128)), writes=["lnb"])
            C1 = 0.5 / ALPHA
            def xload(tt):
                xdep = ["Y%d" % tt] if xsrc is not x_in else []
                S.dma("sp", lambda e: e.dma_start(out=xts[tt % 2], in_=xsrc[tt * 128:(tt + 1) * 128, :]), reads=xdep,
                      writes=["xt%d" % (tt % 2)])

            def stageA(tt):
                xt = xts[tt % 2]; xn = "xt%d" % (tt % 2)
                rp = rps[tt % 5]; rpn = "rp%d" % (tt % 5)
                rb = rbs[tt % 2]; rbn = "rb%d" % (tt % 2)
                for hf in range(2):
                    mm_group([(PB(hf), mergedT[:, c, tt * 128:(tt + 1) * 128], w_o[:, c, hf * 512:(hf + 1) * 512]) for c in range(8)],
                             ["mergedT", "w_o"], ["P:%d" % hf])
                    yield
                    S.op("dve", lambda e: e.scalar_tensor_tensor(
                        out=rp[:, hf * 512:(hf + 1) * 512], in0=PB(hf), scalar=float(C1), in1=xt[:, hf * 512:(hf + 1) * 512],
                        op0=ALU.mult, op1=ALU.add), reads=["P:%d" % hf, xn], writes=[rpn])
                    yield
                S.op("dve", lambda e: e.tensor_copy(out=rb, in_=rp), reads=[rpn], writes=[rbn])
                yield

            def stageB1(tt):
                rb = rbs[tt % 2]; rbn = "rb%d" % (tt % 2)
                rT = rTs[tt % 2]; rTn = "rT%d" % (tt % 2)

                def trr(e):
                    for c in range(8):
                        e.transpose(out=PBb(2)[:, c * 128:(c + 1) * 128], in_=rb[:, c * 128:(c + 1) * 128], identity=ident)
                S.op("pe", trr, reads=[rbn, "ident"], writes=["P:2"])
                yield
                evac(rT, PBb(2).rearrange("p (c t) -> p c t", c=8), ["P:2"], [rTn], eng="act")
                yield

            def stageB2(tt):
                st8b = st8s[tt % 3]; stnb = "st8%d" % (tt % 3)
                rp = rps[tt % 5]; rpn = "rp%d" % (tt % 5)
                rT = rTs[tt % 2]; rTn = "rT%d" % (tt % 2)
                for hf in range(2):
                    mm_group([(PB(5 + hf), pT[:, c, tt * 128:(tt + 1) * 128], w_p[:, c, hf * 512:(hf + 1) * 512]) for c in range(2)],
                             ["pT", "w_p"], ["P:%d" % (5 + hf)])
                    yield
                    S.op("act", lambda e: e.activation(out=pps[hf], in_=PB(5 + hf), func=AF.Copy), reads=["P:%d" % (5 + hf)],
                         writes=["pp%d" % hf])
                    yield
                for hf in range(2):
                    mm_group([(PB(3 + hf), rT[:, c, :], w_pg[:, c, hf * 512:(hf + 1) * 512]) for c in range(8)],
                             [rTn, "w_pg"], ["P:%d" % (3 + hf)])
                    yield
                    S.op("act", lambda e: e.activation(out=ths[hf], in_=PB(3 + hf), func=AF.Tanh, scale=float(0.5 * ALPHA)),
                         reads=["P:%d" % (3 + hf)], writes=["th%d" % hf])
                    yield
                for hf in range(2):
                    S.op("dve", lambda e: e.scalar_tensor_tensor(out=ths[hf], in0=ths[hf], scalar=1.0, in1=pps[hf],
                                                                op0=ALU.add, op1=ALU.mult),
                         reads=["th%d" % hf, "pp%d" % hf], writes=["th%d" % hf])
                    yield
                    S.op("dve", lambda e: e.scalar_tensor_tensor(
                        out=rp[:, hf * 512:(hf + 1) * 512], in0=ths[hf], scalar=float(C1), in1=rp[:, hf * 512:(hf + 1) * 512],
                        op0=ALU.mult, op1=ALU.add, accum_out=st8b[:, 7 * hf:7 * hf + 1]), reads=["th%d" % hf, rpn], writes=[rpn, stnb])
                    yield

            def stageC1(tt):
                rp = rps[tt % 5]; rpn = "rp%d" % (tt % 5)
                y = ys[tt % 3]; yn = "y%d" % (tt % 3)
                st8 = st8s[tt % 3]; stn = "st8%d" % (tt % 3)
                S.op("act", lambda e: e.activation(out=y, in_=rp, func=AF.Square, accum_out=st8[:, 1:2]), reads=[rpn], writes=[yn, stn])
                yield
                S.op("dve", lambda e: e.tensor_tensor(out=st8[:, 0:1], in0=st8[:, 0:1], in1=st8[:, 7:8], op=ALU.add), reads=[stn], writes=[stn])
                yield
                S.op("dve", lambda e: e.tensor_scalar(out=st8[:, 2:3], in0=st8[:, 0:1], scalar1=float(-1.0 / D), scalar2=None, op0=ALU.mult),
                     reads=[stn], writes=[stn])
                yield
                S.op("dve", lambda e: e.tensor_tensor(out=st8[:, 3:4], in0=st8[:, 2:3], in1=st8[:, 2:3], op=ALU.mult), reads=[stn], writes=[stn])
                yield
                S.op("dve", lambda e: e.scalar_tensor_tensor(out=st8[:, 4:5], in0=st8[:, 1:2], scalar=float(1.0 / D), in1=st8[:, 3:4],
                                                            op0=ALU.mult, op1=ALU.subtract), reads=[stn], writes=[stn])
                yield

            def stageC2(tt):
                rp = rps[tt % 5]; rpn = "rp%d" % (tt % 5)
                y = ys[tt % 3]; yn = "y%d" % (tt % 3)
                st8 = st8s[tt % 3]; stn = "st8%d" % (tt % 3)
                S.op("act", lambda e: e.activation(out=st8[:, 5:6], in_=st8[:, 4:5], func=AF.Sqrt, bias=epsb[:, 3:4]), reads=[stn, "epsb"], writes=[stn])
                yield
                yield
                S.op("dve", lambda e: e.reciprocal(out=st8[:, 6:7], in_=st8[:, 5:6]), reads=[stn], writes=[stn])
                yield
                S.op("dve", lambda e: e.tensor_scalar(out=y, in0=rp, scalar1=st8[:, 2:3], scalar2=st8[:, 6:7],
                                                     op0=ALU.add, op1=ALU.mult), reads=[rpn, stn], writes=[yn])
                yield
                S.op("pool", lambda e: e.tensor_tensor(out=y, in0=y, in1=lng, op=ALU.mult), reads=[yn, "lng"], writes=[yn])
                yield
                S.op("pool", lambda e: e.tensor_tensor(out=y, in0=y, in1=lnb, op=ALU.add), reads=[yn, "lnb"], writes=[yn])
                yield
                S.dma("sp", lambda e: e.dma_start(out=ydst[tt * 128:(tt + 1) * 128, :], in_=y), reads=[yn],
                      writes=(["Y%d" % tt] if ydst is not out else []))
                yield

            def rr(*gens):
                gens = [g for g in gens if g is not None]
                while gens:
                    for g in list(gens):
                        try:
                            next(g)
                        except StopIteration:
                            gens.remove(g)

            xload(0)
            xload(1)
            rr(stageA(0))
            for tt in range(19):
                rr(stageC2(tt - 3) if 3 <= tt < 19 else None,
                   stageC1(tt - 2) if 2 <= tt < 18 else None,
                   stageB2(tt - 1) if 1 <= tt < 17 else None,
                   stageA(tt + 1) if tt + 1 < 16 else None,
                   stageB1(tt) if tt < 16 else None)
                if tt + 2 < 16:
                    xload(tt + 2)

        load_consts()
        cur = x_in
        for i, li in enumerate(layers):
            dst = out if i == len(layers) - 1 else scr
            layer(li, cur, dst)
            cur = dst
        S.emit(nc)
    return nc


FUSED = True


def _in_maps(x, p, inputs, consts):
    maps = []
    shared = {n: np.ascontiguousarray(np.asarray(inputs[n], dtype=np.float32)) for n in WNAMES}
    for b in range(NCORES):
        m = {"x": np.ascontiguousarray(x[b]), "p": np.ascontiguousarray(p[:, b])}
        m.update(shared)
        m.update(consts)
        maps.append(m)
    return maps


def kernel(**inputs):
    consts = make_consts()
    x = np.asarray(inputs["x"], dtype=np.float32)
    p = np.asarray(inputs["p"], dtype=np.float32)
    if FUSED:
        nc = build_program([0, 1])
        res = run_bass_kernel_spmd(nc, _in_maps(x, p, inputs, consts), core_ids=list(range(NCORES)))
        return np.stack([np.asarray(r["out"], dtype=np.float32) for r in res.results], axis=0)
    cur = x
    for li in range(DEPTH):
        nc = build_program([li])
        res = run_bass_kernel_spmd(nc, _in_maps(cur, p, inputs, consts), core_ids=list(range(NCORES)))
        cur = np.stack([np.asarray(r["out"], dtype=np.float32) for r in res.results], axis=0)
    return cur
```
